# Optimizing a Trainium2 kernel written in Bass

```python
import math
import jax, jax.numpy as jnp
from jax import lax
import numpy as np

D_MODEL = 1024
BATCH = 8
SEQ = 8192
DEPTH = 2

D_MIX = D_MODEL
A_HEADS = 8
A_NOPE = 64
A_ROPE = 32
A_V = 64
A_Q_RANK = 384
A_KV_RANK = 256
B_HEADS = 8
B_KV_HEADS = 2
B_HEAD_DIM = 64
B_GROUP = B_HEADS // B_KV_HEADS
WINDOW = 128
BLOCK = 128
REL_BUCKETS = 32
REL_MAX_DIST = 128
D_FF = 2816
FFN_RES_WEIGHT = 0.5
ROPE_THETA = 10000.0
EPS = 1e-6
Q_BLOCK = 128
NEG_INF = -1e30

IN_COLS = A_Q_RANK + A_KV_RANK + A_ROPE + B_HEADS * B_HEAD_DIM + 2 * B_KV_HEADS * B_HEAD_DIM

kernel_name = "hybrid_mla_swa_macaron_encoder"


def rmsnorm(x, g):
    xf = x.astype(jnp.float32)
    y = xf * lax.rsqrt(jnp.mean(xf * xf, axis=-1, keepdims=True) + EPS)
    return (y * g.astype(jnp.float32)).astype(x.dtype)


def swiglu(x, w_gate, w_up, w_down):
    return (jax.nn.silu(x @ w_gate) * (x @ w_up)) @ w_down


def rope_tables(seq):
    pos = jnp.arange(seq, dtype=jnp.float32)
    inv = ROPE_THETA ** (-jnp.arange(0, A_ROPE, 2, dtype=jnp.float32) / A_ROPE)
    ang = pos[:, None] * inv[None, :]
    return jnp.cos(ang), jnp.sin(ang)


def apply_rope(x, cos, sin):
    x1, x2 = jnp.split(x, 2, axis=-1)
    out = jnp.concatenate([x1 * cos - x2 * sin, x2 * cos + x1 * sin], axis=-1)
    return out.astype(x.dtype)


def t5_bucket(rel):
    nb = REL_BUCKETS // 2
    max_exact = nb // 2
    bucket = jnp.where(rel > 0, nb, 0)
    n = jnp.abs(rel)
    nf = jnp.maximum(n, 1).astype(jnp.float32)
    large = max_exact + (jnp.log(nf / max_exact) / math.log(REL_MAX_DIST / max_exact)
                         * (nb - max_exact)).astype(jnp.int32)
    large = jnp.minimum(large, nb - 1)
    return bucket + jnp.where(n < max_exact, n, large)


def band_bias(rel_bias):
    r = jnp.arange(BLOCK)[:, None]
    j = jnp.arange(3 * BLOCK)[None, :]
    rel = j - BLOCK - r
    bias = rel_bias[t5_bucket(rel)]
    bias = jnp.transpose(bias, (2, 0, 1)).reshape(B_KV_HEADS, B_GROUP, BLOCK, 3 * BLOCK)
    in_win = jnp.abs(rel) <= WINDOW
    return bias.astype(jnp.float32), in_win


def mla_attention(c_q, c_kv, k_rope, q_norm_g, w_uq, kv_norm_g, w_ukv, cos, sin):
    B, S, _ = c_q.shape
    q = (rmsnorm(c_q, q_norm_g) @ w_uq).reshape(B, S, A_HEADS, A_NOPE + A_ROPE)
    q_nope, q_rope = q[..., :A_NOPE], q[..., A_NOPE:]
    q_rope = apply_rope(q_rope, cos[None, :, None, :], sin[None, :, None, :])
    kv = (rmsnorm(c_kv, kv_norm_g) @ w_ukv).reshape(B, S, A_HEADS, A_NOPE + A_V)
    k_nope, v = kv[..., :A_NOPE], kv[..., A_NOPE:]
    k_rope = apply_rope(k_rope, cos[None], sin[None])
    scale = (A_NOPE + A_ROPE) ** -0.5
    nblk = S // Q_BLOCK
    qn_b = q_nope.reshape(B, nblk, Q_BLOCK, A_HEADS, A_NOPE).transpose(1, 0, 2, 3, 4)
    qr_b = q_rope.reshape(B, nblk, Q_BLOCK, A_HEADS, A_ROPE).transpose(1, 0, 2, 3, 4)

    def attend(blk):
        qn, qr = blk
        s = (jnp.einsum('bqhd,bkhd->bhqk', qn, k_nope)
             + jnp.einsum('bqhr,bkr->bhqk', qr, k_rope))
        p = jax.nn.softmax(s.astype(jnp.float32) * scale, axis=-1).astype(v.dtype)
        return jnp.einsum('bhqk,bkhd->bqhd', p, v)

    out = lax.map(attend, (qn_b, qr_b))
    return out.transpose(1, 0, 2, 3, 4).reshape(B, S, A_HEADS * A_V)


def window_gqa(q, k, v, sink, bias, in_win):
    B, S, _ = q.shape
    nblk = S // BLOCK
    qb = q.reshape(B, nblk, BLOCK, B_KV_HEADS, B_GROUP, B_HEAD_DIM)

    def banded(t):
        t = t.reshape(B, S, B_KV_HEADS, B_HEAD_DIM)
        tp = jnp.pad(t, ((0, 0), (BLOCK, BLOCK), (0, 0), (0, 0)))
        tp = tp.reshape(B, nblk + 2, BLOCK, B_KV_HEADS, B_HEAD_DIM)
        return jnp.concatenate([tp[:, :-2], tp[:, 1:-1], tp[:, 2:]], axis=2)

    kw, vw = banded(k), banded(v)
    scale = B_HEAD_DIM ** -0.5
    s = jnp.einsum('bnqkgd,bnjkd->bnkgqj', qb, kw).astype(jnp.float32) * scale
    s = s + bias[None, None]
    key_pos = (jnp.arange(nblk)[:, None] - 1) * BLOCK + jnp.arange(3 * BLOCK)[None, :]
    valid = (key_pos >= 0) & (key_pos < S)
    mask = in_win[None, :, :] & valid[:, None, :]
    s = jnp.where(mask[None, :, None, None], s, NEG_INF)
    sk = sink.astype(jnp.float32).reshape(B_KV_HEADS, B_GROUP)[None, None, :, :, None, None]
    m = jnp.maximum(jnp.max(s, axis=-1, keepdims=True), sk)
    p = jnp.exp(s - m)
    p = p / (jnp.sum(p, axis=-1, keepdims=True) + jnp.exp(sk - m))
    out = jnp.einsum('bnkgqj,bnjkd->bnqkgd', p.astype(vw.dtype), vw)
    return out.reshape(B, S, B_HEADS * B_HEAD_DIM)


def setup_inputs(seed: int = 0) -> dict:
    key = jax.random.key(seed)
    ks = iter(jax.random.split(key, 32))

    def w(shape, fan_in):
        return jax.random.normal(next(ks), shape, jnp.float32) * fan_in ** -0.5

    def g(shape):
        return 1.0 + 0.05 * jax.random.normal(next(ks), shape, jnp.float32)

    L = DEPTH
    return {
        "x": jax.random.normal(next(ks), (BATCH, SEQ, D_MODEL), jnp.float32),
        "rel_bias": 0.1 * jax.random.normal(next(ks), (REL_BUCKETS, B_HEADS), jnp.float32),
        "ffn1_pre_g": g((L, D_MODEL)),
        "ffn1_w_gate": w((L, D_MODEL, D_FF), D_MODEL),
        "ffn1_w_up": w((L, D_MODEL, D_FF), D_MODEL),
        "ffn1_w_down": w((L, D_FF, D_MODEL), D_FF),
        "ffn1_post_g": g((L, D_MODEL)),
        "mix_pre_g": g((L, D_MODEL)),
        "w_in": w((L, D_MODEL, IN_COLS), D_MODEL),
        "mla_q_norm_g": g((L, A_Q_RANK)),
        "mla_w_uq": w((L, A_Q_RANK, A_HEADS * (A_NOPE + A_ROPE)), A_Q_RANK),
        "mla_kv_norm_g": g((L, A_KV_RANK)),
        "mla_w_ukv": w((L, A_KV_RANK, A_HEADS * (A_NOPE + A_V)), A_KV_RANK),
        "swa_sink": 0.5 * jax.random.normal(next(ks), (L, B_HEADS), jnp.float32),
        "w_out": w((L, D_MIX, D_MODEL), D_MIX),
        "mix_post_g": g((L, D_MODEL)),
        "ffn2_pre_g": g((L, D_MODEL)),
        "ffn2_w_gate": w((L, D_MODEL, D_FF), D_MODEL),
        "ffn2_w_up": w((L, D_MODEL, D_FF), D_MODEL),
        "ffn2_w_down": w((L, D_FF, D_MODEL), D_FF),
        "ffn2_post_g": g((L, D_MODEL)),
    }


def reference(x, rel_bias, ffn1_pre_g, ffn1_w_gate, ffn1_w_up, ffn1_w_down, ffn1_post_g,
              mix_pre_g, w_in, mla_q_norm_g, mla_w_uq, mla_kv_norm_g, mla_w_ukv, swa_sink,
              w_out, mix_post_g, ffn2_pre_g, ffn2_w_gate, ffn2_w_up, ffn2_w_down, ffn2_post_g):
    S = x.shape[1]
    cos, sin = rope_tables(S)
    bias, in_win = band_bias(rel_bias)
    splits = list(np.cumsum([A_Q_RANK, A_KV_RANK, A_ROPE,
                             B_HEADS * B_HEAD_DIM, B_KV_HEADS * B_HEAD_DIM]))
    for i in range(DEPTH):
        h = rmsnorm(x, ffn1_pre_g[i])
        x = x + FFN_RES_WEIGHT * rmsnorm(swiglu(h, ffn1_w_gate[i], ffn1_w_up[i], ffn1_w_down[i]),
                                         ffn1_post_g[i])
        h = rmsnorm(x, mix_pre_g[i])
        z = h @ w_in[i]
        c_q, c_kv, k_rope, q_b, k_b, v_b = jnp.split(z, splits, axis=-1)
        o_a = mla_attention(c_q, c_kv, k_rope, mla_q_norm_g[i], mla_w_uq[i],
                            mla_kv_norm_g[i], mla_w_ukv[i], cos, sin)
        o_b = window_gqa(q_b, k_b, v_b, swa_sink[i], bias, in_win)
        o = jnp.concatenate([o_a, o_b], axis=-1) @ w_out[i]
        x = x + rmsnorm(o, mix_post_g[i])
        h = rmsnorm(x, ffn2_pre_g[i])
        x = x + FFN_RES_WEIGHT * rmsnorm(swiglu(h, ffn2_w_gate[i], ffn2_w_up[i], ffn2_w_down[i]),
                                         ffn2_post_g[i])
    return x
```

```python
import os
import numpy as np
import concourse.bass as bass
import concourse.mybir as mybir
from concourse.bass_utils import run_bass_kernel_spmd

F32 = mybir.dt.float32
BF16 = mybir.dt.bfloat16
ALU = mybir.AluOpType
AF = mybir.ActivationFunctionType

ENGS = ("pe", "act", "dve", "pool", "sp")

L = 2
S_LEN = 8192
D = 1024
DFF = 2816
NF = 22
TT = 512
NT = S_LEN // TT
EPS = 1e-6
PARTS = os.environ.get('DBG_PARTS', 'cdeqrkvQKV')


class Buf:
    __slots__ = ("last_w", "readers", "excl")

    def __init__(self, excl=False):
        self.last_w = None
        self.readers = []
        self.excl = excl


class Rec:
    __slots__ = ("eng", "idx", "fn", "deps", "signal", "semval", "is_dma", "dsem", "dval", "prev_dma")

    def __init__(self, eng, idx, fn, deps, is_dma):
        self.eng = eng
        self.idx = idx
        self.fn = fn
        self.deps = deps
        self.signal = False
        self.semval = 0
        self.is_dma = is_dma
        self.dsem = None
        self.dval = 0
        self.prev_dma = None


class Sched:
    def __init__(self, nc, n_dma_sems=12):
        self.nc = nc
        self.ops = {e: [] for e in ENGS}
        self.n_dma_sems = n_dma_sems
        self.dma_hist = {e: [] for e in ENGS}

    @staticmethod
    def _merge(deps, rec):
        if rec.is_dma:
            deps[id(rec)] = rec
        else:
            cur = deps.get(rec.eng)
            if cur is None or cur.idx < rec.idx:
                deps[rec.eng] = rec

    def _add(self, eng, fn, reads, writes, is_dma):
        ex = [b for b in reads if b.excl]
        if ex:
            reads = [b for b in reads if not b.excl]
            writes = list(writes) + [b for b in ex if b not in writes]
        deps = {}
        for b in reads:
            if b.last_w is not None:
                self._merge(deps, b.last_w)
        for b in writes:
            if b.last_w is not None:
                self._merge(deps, b.last_w)
            for r in b.readers:
                self._merge(deps, r)
        rec = Rec(eng, len(self.ops[eng]), fn, list(deps.values()), is_dma)
        self.ops[eng].append(rec)
        for b in reads:
            if not is_dma:
                rl = b.readers
                for i, r in enumerate(rl):
                    if (not r.is_dma) and r.eng == eng:
                        rl[i] = rec
                        break
                else:
                    rl.append(rec)
            else:
                b.readers.append(rec)
        for b in writes:
            b.last_w = rec
            b.readers = []
        return rec

    def op(self, eng, fn, reads=(), writes=()):
        return self._add(eng, fn, reads, writes, False)

    def dma(self, eng, fn, reads=(), writes=()):
        rec = self._add(eng, fn, reads, writes, True)
        hist = self.dma_hist[eng]
        k = len(hist)
        rec.dsem = k % self.n_dma_sems
        rec.dval = 16 * (k // self.n_dma_sems + 1)
        if k >= self.n_dma_sems:
            rec.prev_dma = hist[k - self.n_dma_sems]
        hist.append(rec)
        return rec

    def barrier(self):
        deps = []
        for e in ENGS:
            for r in reversed(self.ops[e]):
                if (not r.is_dma) and r.fn is not None:
                    deps.append(r)
                    break
            deps += self.dma_hist[e][-self.n_dma_sems:]
        for e in ENGS:
            rec = Rec(e, len(self.ops[e]), None, list(deps), False)
            self.ops[e].append(rec)

    def emit(self):
        nc = self.nc
        for e in ENGS:
            for rec in self.ops[e]:
                for d in rec.deps:
                    if d.is_dma:
                        continue
                    if d.eng == rec.eng and rec.eng == "pe" and not rec.is_dma:
                        continue
                    d.signal = True
        for e in ENGS:
            c = 0
            for rec in self.ops[e]:
                if rec.signal and not rec.is_dma:
                    c += 1
                    rec.semval = c
        sems = {e: nc.alloc_semaphore(name=f"s_{e}") for e in ENGS}
        dsems = {e: [nc.alloc_semaphore(name=f"d_{e}{i}") for i in range(self.n_dma_sems)]
                 for e in ENGS if self.dma_hist[e]}
        engattr = {"pe": "tensor", "act": "scalar", "dve": "vector", "pool": "gpsimd", "sp": "sync"}
        sched = self

        def actions(e):
            known = {x: 0 for x in ENGS}
            dknown = {}
            for rec in sched.ops[e]:
                need = {}
                dneed = {}
                for d in rec.deps:
                    if d.is_dma:
                        key = (d.eng, d.dsem)
                        if dknown.get(key, 0) < d.dval:
                            dneed[key] = max(dneed.get(key, 0), d.dval)
                    else:
                        if d.eng == e and e == "pe" and rec.fn is not None:
                            continue
                        if known[d.eng] < d.semval:
                            need[d.eng] = max(need.get(d.eng, 0), d.semval)
                if rec.is_dma and rec.prev_dma is not None:
                    p = rec.prev_dma
                    key = (p.eng, p.dsem)
                    if dknown.get(key, 0) < p.dval:
                        dneed[key] = max(dneed.get(key, 0), p.dval)
                for x, v in need.items():
                    yield ("wait", x, v)
                    known[x] = v
                for key, v in dneed.items():
                    yield ("wait", key, v)
                    dknown[key] = v
                if rec.fn is None:
                    continue
                if rec.is_dma:
                    yield ("op", rec, (e, rec.dsem), 16)
                elif rec.signal:
                    yield ("op", rec, e, 1)
                else:
                    yield ("op", rec, None, 0)
            last = {}
            for r in sched.dma_hist[e]:
                last[r.dsem] = max(last.get(r.dsem, 0), r.dval)
            for si, v in last.items():
                if dknown.get((e, si), 0) < v:
                    yield ("wait", (e, si), v)

        if os.environ.get("DBG_SIM"):
            streams = {e: list(actions(e)) for e in ENGS}
            ptr = {e: 0 for e in ENGS}
            val = {}
            print("ops per engine", {e: len(streams[e]) for e in ENGS})
            while True:
                prog = False
                for e in ENGS:
                    st = streams[e]
                    while ptr[e] < len(st):
                        a = st[ptr[e]]
                        if a[0] == "wait":
                            if val.get(a[1], 0) >= a[2]:
                                ptr[e] += 1
                                prog = True
                            else:
                                break
                        else:
                            if a[2] is not None:
                                val[a[2]] = val.get(a[2], 0) + a[3]
                            ptr[e] += 1
                            prog = True
                if not prog:
                    break
            stuck = {e: (ptr[e], len(streams[e]), streams[e][ptr[e]][:3] if ptr[e] < len(streams[e]) else None) for e in ENGS}
            print("SIM end:", stuck)

        def sem_of(key):
            return sems[key] if isinstance(key, str) else dsems[key[0]][key[1]]

        def run_engine(e, eng):
            for a in actions(e):
                if a[0] == "wait":
                    eng.wait_ge(sem_of(a[1]), a[2])
                else:
                    ins = a[1].fn(eng)
                    if a[2] is not None:
                        ins.then_inc(sem_of(a[2]), a[3])

        with nc.Block() as block:
            for e in ENGS:
                if not sched.ops[e]:
                    continue
                getattr(block, engattr[e])(lambda eng, e=e: run_engine(e, eng))


class Arena:
    def __init__(self, ap, cap_bytes):
        self.ap = ap
        self.cap = cap_bytes
        self.off = 0

    def reset(self):
        self.off = 0

    def alloc(self, nelem, dt):
        nb = nelem * (4 if dt == F32 else 2)
        nb = (nb + 63) // 64 * 64
        a = self.off
        self.off += nb
        assert self.off <= self.cap, (self.off, self.cap)
        v = self.ap[:, a // 2:(a + nb) // 2]
        if dt == F32:
            v = v.bitcast(F32)
        return v[:, :nelem]


def build(debug=False, upto=99):
    nc = bass.Bass("TRN2", target_bir_lowering=False)
    S = Sched(nc)

    def din(name, shape, dt=F32):
        return nc.dram_tensor(name, list(shape), dt, kind="ExternalInput").ap()

    def dscr(name, shape, dt):
        return nc.dram_tensor(name, list(shape), dt, kind=("ExternalOutput" if debug else "Internal")).ap()

    x_d = din("x", [S_LEN, D])
    out_d = nc.dram_tensor("out", [S_LEN, D], F32, kind="ExternalOutput").ap()
    g_d = din("g_all", [L, 6, D])
    gq_d = din("gq", [L, 128, 3])
    gkv_d = din("gkv", [L, 128, 2])
    sink_d = din("sink", [L, 8])
    wgate_d = [din("ffn1_w_gate", [L, D, DFF]), din("ffn2_w_gate", [L, D, DFF])]
    wup_d = [din("ffn1_w_up", [L, D, DFF]), din("ffn2_w_up", [L, D, DFF])]
    wdown_d = [din("ffn1_w_down", [L, DFF, D]), din("ffn2_w_down", [L, DFF, D])]
    winx_d = din("w_inx", [L, D, 1536])
    wuqa_d = din("w_uqa", [L, 384, 768])
    wuqb_d = din("w_uqb", [L, 384, 768])
    wuk_d = din("w_uk", [L, 256, 512])
    wuv_d = din("w_uv", [L, 256, 512])
    wout_d = din("w_out", [L, D, D])
    tabs_d = din("tabs", [2, 128, S_LEN])
    biasT_d = din("biasT", [2, 128, 1536])
    eye_d = din("eye", [128, 128])

    def wscr(name, shape):
        return nc.dram_tensor(name, list(shape), BF16, kind="Internal").ap()
    wgu_s = [[wscr(f"wgu_s{l}{f}", [11, 128, 4096]) for f in range(2)] for l in range(L)]
    wd_s = [[wscr(f"wd_s{l}{f}", [DFF, D]) for f in range(2)] for l in range(L)]
    winx_s = [wscr(f"winx_s{l}", [D, 1536]) for l in range(L)]
    wuqa_s = [wscr(f"wuqa_s{l}", [384, 768]) for l in range(L)]
    wuqb_s = [wscr(f"wuqb_s{l}", [384, 768]) for l in range(L)]
    wuk_s = [wscr(f"wuk_s{l}", [256, 512]) for l in range(L)]
    wuv_s = [wscr(f"wuv_s{l}", [256, 512]) for l in range(L)]
    wout_s = [wscr(f"wout_s{l}", [D, D]) for l in range(L)]
    XS = dscr("XS", [S_LEN, D], F32)
    QT = dscr("QT", [8, 96, S_LEN], BF16)
    KT = dscr("KT", [8, 64, S_LEN], BF16)
    KR = dscr("KR", [32, S_LEN], BF16)
    VA = dscr("VA", [S_LEN, 520], BF16)
    QB = dscr("QB", [8, 64, S_LEN], BF16)
    KB = dscr("KB", [2, 64, S_LEN], BF16)
    VB = dscr("VB", [S_LEN, 130], BF16)
    OT = dscr("OT", [D, S_LEN], BF16)
    xs_b = [Buf() for _ in range(NT)]
    xin_b = [Buf() for _ in range(NT)]
    out_b = [Buf() for _ in range(NT)]
    qt_b = [[Buf() for _ in range(NT)] for _ in range(8)]
    kt_b = [[Buf() for _ in range(NT)] for _ in range(8)]
    kr_b = [Buf() for _ in range(NT)]
    va_b = [Buf() for _ in range(NT)]
    qb_b = [[Buf() for _ in range(NT)] for _ in range(8)]
    kb_b = [[Buf() for _ in range(NT)] for _ in range(2)]
    vb_b = [Buf() for _ in range(NT)]
    ot_b = [[Buf() for _ in range(NT)] for _ in range(8)]

    ident = nc.alloc_sbuf_tensor("ident", [128, 128], BF16).ap()
    identf = nc.alloc_sbuf_tensor("identf", [128, 128], F32).ap()
    esel = nc.alloc_sbuf_tensor("esel", [128, 64], F32).ap()
    onesf = nc.alloc_sbuf_tensor("onesf", [128, 128], BF16).ap()
    eselb = nc.alloc_sbuf_tensor("eselb", [128, 64], BF16).ap()
    epsb = nc.alloc_sbuf_tensor("epsb", [128, 1], F32).ap()
    ARENA_BYTES = 200 * 1024
    arena_t = nc.alloc_sbuf_tensor("arena", [128, ARENA_BYTES // 2], BF16).ap()
    A = Arena(arena_t, ARENA_BYTES)
    const_b = Buf()

    PSA = nc.alloc_psum_tensor("PSA", [128, 8 * 512], F32).ap()
    P01 = PSA[:, 0:1024]
    P23 = PSA[:, 1024:2048]
    P45 = PSA[:, 2048:3072]
    banks = [PSA[:, i * 512:(i + 1) * 512] for i in range(8)]
    bank_b = [Buf(excl=True) for _ in range(8)]

    S.dma("sp", lambda e: e.dma_start(out=identf, in_=eye_d), writes=[const_b])
    S.op("dve", lambda e: e.tensor_copy(out=ident, in_=identf), reads=[const_b], writes=[const_b])
    S.op("dve", lambda e: e.memset(esel, 0.0), writes=[const_b])
    S.op("dve", lambda e: e.memset(esel[64:65, :], 1.0), reads=[const_b], writes=[const_b])
    S.op("dve", lambda e: e.memset(onesf, 1.0), reads=[const_b], writes=[const_b])
    S.op("dve", lambda e: e.memset(eselb, 1.0), reads=[const_b], writes=[const_b])
    S.op("dve", lambda e: e.memset(epsb, EPS), reads=[const_b], writes=[const_b])

    wb = {}

    def cast_dma(key, dst, src):
        b = Buf()
        S.dma("pool", lambda e: e.dma_start(out=dst, in_=src), writes=[b])
        wb.setdefault(key, []).append(b)

    for l in range(L):
        if l * 10 > upto:
            break
        for f in range(2):
            for gu, wsrc in enumerate((wgate_d[f], wup_d[f])):
                for k in range(8):
                    src = wsrc[l, k * 128:(k + 1) * 128, :].rearrange("p (g j) -> p g j", g=11)
                    dst = wgu_s[l][f].rearrange("g p (u k j) -> p g u k j", u=2, k=8)[:, :, gu, k, :]
                    cast_dma(("wgu", l, f), dst, src)
            for r in range(0, DFF, 704):
                cast_dma(("wd", l, f), wd_s[l][f][r:r + 704, :], wdown_d[f][l, r:r + 704, :])
        for r in range(0, D, 256):
            cast_dma(("winx", l), winx_s[l][r:r + 256, :], winx_d[l, r:r + 256, :])
            cast_dma(("wout", l), wout_s[l][r:r + 256, :], wout_d[l, r:r + 256, :])
        cast_dma(("wuqa", l), wuqa_s[l], wuqa_d[l])
        cast_dma(("wuqb", l), wuqb_s[l], wuqb_d[l])
        cast_dma(("wuk", l), wuk_s[l], wuk_d[l])
        cast_dma(("wuv", l), wuv_s[l], wuv_d[l])

    def rsqrt_rows(out_ap, in_ap, scale, rbufs, wbufs):
        S.op("act", lambda e: e.activation(out=out_ap, in_=in_ap, func=AF.Sqrt, scale=scale, bias=epsb[0:out_ap.shape[0], :]),
             reads=rbufs, writes=wbufs)
        S.op("dve", lambda e: e.reciprocal(out=out_ap, in_=out_ap), reads=wbufs, writes=wbufs)

    def load_gain(dst, idx_l, idx_g, buf, half=False):
        S.dma("sp", lambda e: e.dma_start(out=dst, in_=g_d[idx_l, idx_g].partition_broadcast(128)), writes=[buf])
        if half:
            S.op("act", lambda e: e.mul(out=dst, in_=dst, mul=0.5), reads=[buf], writes=[buf])

    def norm_rows(xap, xbuf, gbc, gbuf, ss, ssb, rs, rsb, out_ap, out_buf, junk):
        S.op("act", lambda e: e.activation(out=junk, in_=xap, func=AF.Square, accum_out=ss), reads=[xbuf], writes=[ssb])
        rsqrt_rows(rs, ss, 1.0 / D, [ssb], [rsb])
        S.op("dve", lambda e: e.scalar_tensor_tensor(out=out_ap, in0=xap, scalar=rs, in1=gbc, op0=ALU.mult, op1=ALU.mult),
             reads=[xbuf, rsb, gbuf], writes=[out_buf])

    def transposes(hb_s, hb_buf, hT, hT_buf, s, pbank):
        pt = banks[pbank].bitcast(BF16).rearrange("p (k j) -> p k j", k=8)
        for k in range(8):
            S.op("pe", lambda e, k=k: e.transpose(out=pt[:, k, :], in_=hb_s[:, k * 128:(k + 1) * 128], identity=ident),
                 reads=[hb_buf, const_b], writes=[bank_b[pbank]])
        eng = "dve" if s % 2 == 0 else "act"
        if eng == "dve":
            S.op("dve", lambda e: e.tensor_copy(out=hT[:, :, s * 128:(s + 1) * 128], in_=pt), reads=[bank_b[pbank]], writes=[hT_buf])
        else:
            S.op("act", lambda e: e.copy(out=hT[:, :, s * 128:(s + 1) * 128], in_=pt), reads=[bank_b[pbank]], writes=[hT_buf])

    def post_norm_add(py, pyb, s, gbc, gbuf, ss, ssb, rs, rsb, tn, tnb, xs_ap, xs_buf, junk):
        S.op("act", lambda e: e.activation(out=junk, in_=py, func=AF.Square, accum_out=ss), reads=pyb, writes=[ssb])
        rsqrt_rows(rs, ss, 1.0 / D, [ssb], [rsb])
        S.op("dve", lambda e: e.scalar_tensor_tensor(out=tn, in0=py, scalar=rs, in1=gbc, op0=ALU.mult, op1=ALU.mult),
             reads=pyb + [rsb, gbuf], writes=[tnb])
        S.op("pool", lambda e: e.tensor_tensor(out=xs_ap, in0=xs_ap, in1=tn, op=ALU.add), reads=[tnb, xs_buf], writes=[xs_buf])

    def xtile_src(dram, t):
        return dram[t * TT:(t + 1) * TT, :].rearrange("(s p) d -> p s d", p=128)

    def ffn_pass(l, f, src, src_b, dst, dst_b, gi_pre, gi_post):
        S.barrier()
        A.reset()
        wd = A.alloc(NF * 1024, BF16).rearrange("p (f n) -> p f n", f=NF)
        wd_b = Buf()
        NW = 4
        wgu = [A.alloc(4096, BF16).rearrange("p (u k j) -> p u k j", u=2, k=8) for _ in range(NW)]
        wgu_b = [Buf() for _ in range(NW)]
        xt = [A.alloc(4096, F32).rearrange("p (s d) -> p s d", s=4) for _ in range(2)]
        xt_b = [[Buf() for _ in range(4)] for _ in range(2)]
        hb = A.alloc(4096, BF16).rearrange("p (s d) -> p s d", s=4)
        hb_b = [Buf() for _ in range(4)]
        hT = [A.alloc(4096, BF16).rearrange("p (k t) -> p k t", k=8) for _ in range(2)]
        hT_b = [[Buf() for _ in range(4)] for _ in range(2)]
        aT = A.alloc(NF * 512, BF16).rearrange("p (f t) -> p f t", f=NF)
        aT_b = [Buf() for _ in range(NF)]
        gpre = A.alloc(1024, F32)
        gpost = A.alloc(1024, F32)
        gpre_b, gpost_b = Buf(), Buf()
        junk = A.alloc(1024, BF16)
        sg = [A.alloc(512, F32) for _ in range(2)]
        sg_b = [Buf(), Buf()]
        tn = [A.alloc(1024, F32) for _ in range(2)]
        tn_b = [Buf(), Buf()]
        ssA = [[A.alloc(1, F32) for _ in range(4)] for _ in range(2)]
        rsA = [[A.alloc(1, F32) for _ in range(4)] for _ in range(2)]
        ssA_b = [[Buf() for _ in range(4)] for _ in range(2)]
        rsA_b = [[Buf() for _ in range(4)] for _ in range(2)]
        ssB = [A.alloc(1, F32) for _ in range(4)]
        rsB = [A.alloc(1, F32) for _ in range(4)]
        ssB_b = [Buf() for _ in range(4)]
        rsB_b = [Buf() for _ in range(4)]

        wsrc = wd_s[l][f].rearrange("(f p) n -> p f n", p=128)
        for c in range(2):
            S.dma("sp", lambda e, c=c: e.dma_start(out=wd[:, c * 11:(c + 1) * 11, :], in_=wsrc[:, c * 11:(c + 1) * 11, :]),
                  reads=wb[("wd", l, f)], writes=[wd_b])
        load_gain(gpre, l, gi_pre, gpre_b)
        load_gain(gpost, l, gi_post, gpost_b, half=True)

        def load_x(t):
            S.dma("sp", lambda e: e.dma_start(out=xt[t % 2], in_=xtile_src(src, t)), reads=[src_b[t]], writes=xt_b[t % 2])

        def load_w(q):
            g = q % 11
            S.dma("sp", lambda e: e.dma_start(out=wgu[q % NW].rearrange("p u k j -> p (u k j)"), in_=wgu_s[l][f][g]),
                  reads=wb[("wgu", l, f)], writes=[wgu_b[q % NW]])

        def norm_part(t):
            for s in range(4):
                norm_rows(xt[t % 2][:, s, :], xt_b[t % 2][s], gpre, gpre_b, ssA[t % 2][s], ssA_b[t % 2][s],
                          rsA[t % 2][s], rsA_b[t % 2][s], hb[:, s, :], hb_b[s], junk)

        def trans_part(t):
            for s in range(4):
                transposes(hb[:, s, :], hb_b[s], hT[t % 2], hT_b[t % 2][s], s, 6 + s % 2)

        NQ = NT * 11
        load_x(0)
        for q in range(min(3, NQ)):
            load_w(q)
        norm_part(0)
        trans_part(0)
        for t in range(NT):
            if t + 1 < NT:
                load_x(t + 1)
            for grp in range(11):
                q = t * 11 + grp
                if q + 3 < NQ:
                    load_w(q + 3)
                w = wgu[q % NW]
                for j in range(2):
                    fi = grp * 2 + j
                    pg, pgb = banks[fi % 2], bank_b[fi % 2]
                    pu, pub = banks[2 + fi % 2], bank_b[2 + fi % 2]
                    for k in range(8):
                        S.op("pe", lambda e, k=k, j=j, w=w, pg=pg, t=t: e.matmul(
                            pg, lhsT=w[:, 0, k, j * 128:(j + 1) * 128], rhs=hT[t % 2][:, k, :], start=(k == 0), stop=(k == 7)),
                            reads=[wgu_b[q % NW]] + hT_b[t % 2], writes=[pgb])
                    for k in range(8):
                        S.op("pe", lambda e, k=k, j=j, w=w, pu=pu, t=t: e.matmul(
                            pu, lhsT=w[:, 1, k, j * 128:(j + 1) * 128], rhs=hT[t % 2][:, k, :], start=(k == 0), stop=(k == 7)),
                            reads=[wgu_b[q % NW]] + hT_b[t % 2], writes=[pub])
                    sgi, sgb = sg[fi % 2], sg_b[fi % 2]
                    S.op("act", lambda e, pg=pg, sgi=sgi: e.activation(out=sgi, in_=pg, func=AF.Silu), reads=[pgb], writes=[sgb])
                    S.op("dve", lambda e, pu=pu, sgi=sgi, fi=fi: e.tensor_tensor(out=aT[:, fi, :], in0=sgi, in1=pu, op=ALU.mult),
                         reads=[sgb, pub], writes=[aT_b[fi]])
            if t + 1 < NT:
                norm_part(t + 1)
            for s in range(4):
                pyi = 4 if s % 2 == 0 else 0
                py = (P45 if s % 2 == 0 else P01)
                pyb = [bank_b[pyi], bank_b[pyi + 1]]
                for half in range(2):
                    for fi in range(NF):
                        S.op("pe", lambda e, s=s, half=half, fi=fi, py=py: e.matmul(
                            py[:, half * 512:(half + 1) * 512], lhsT=aT[:, fi, s * 128:(s + 1) * 128],
                            rhs=wd[:, fi, half * 512:(half + 1) * 512], start=(fi == 0), stop=(fi == NF - 1)),
                            reads=[aT_b[fi], wd_b], writes=[pyb[half]])
                post_norm_add(py, pyb, s, gpost, gpost_b, ssB[s], ssB_b[s], rsB[s], rsB_b[s], tn[s % 2], tn_b[s % 2],
                              xt[t % 2][:, s, :], xt_b[t % 2][s], junk)
            S.dma("pool", lambda e, t=t: e.dma_start(out=xtile_src(dst, t), in_=xt[t % 2]), reads=xt_b[t % 2], writes=[dst_b[t]])
            if t + 1 < NT:
                trans_part(t + 1)

    def proj_pass(l):
        S.barrier()
        A.reset()
        winx = A.alloc(8 * 1536, BF16).rearrange("p (k n) -> p k n", k=8)
        wuqa = A.alloc(3 * 768, BF16).rearrange("p (k n) -> p k n", k=3)
        wuqb = A.alloc(3 * 768, BF16).rearrange("p (k n) -> p k n", k=3)
        wuk = A.alloc(2 * 512, BF16).rearrange("p (k n) -> p k n", k=2)
        wuv = A.alloc(2 * 512, BF16).rearrange("p (k n) -> p k n", k=2)
        w_b = Buf()
        gq = A.alloc(3, F32)
        gkv = A.alloc(2, F32)
        gmix = A.alloc(1024, F32)
        gmix_b = Buf()
        xt = [A.alloc(4096, F32).rearrange("p (s d) -> p s d", s=4) for _ in range(2)]
        xt_b = [[Buf() for _ in range(4)] for _ in range(2)]
        cs = [A.alloc(1024, F32).rearrange("p (c t) -> p c t", c=2) for _ in range(2)]
        cs_b = [Buf(), Buf()]
        hb = A.alloc(4096, BF16).rearrange("p (s d) -> p s d", s=4)
        hb_b = [Buf() for _ in range(4)]
        hT = [A.alloc(4096, BF16).rearrange("p (k t) -> p k t", k=8) for _ in range(2)]
        hT_b = [[Buf() for _ in range(4)] for _ in range(2)]
        junk = A.alloc(1024, BF16)
        ss = [A.alloc(1, F32) for _ in range(4)]
        rs = [A.alloc(1, F32) for _ in range(4)]
        ss_b = [Buf() for _ in range(4)]
        rs_b = [Buf() for _ in range(4)]
        cT = A.alloc(5 * 512, F32).rearrange("p (c t) -> p c t", c=5)
        sq = A.alloc(5 * 512, BF16).rearrange("p (c t) -> p c t", c=5)
        cT_b = [Buf() for _ in range(5)]
        sq_b = [Buf() for _ in range(5)]
        rsc = A.alloc(2 * 512, F32).rearrange("p (c t) -> p c t", c=2)
        rsc_b = [Buf(), Buf()]
        cn = A.alloc(5 * 512, BF16).rearrange("p (c t) -> p c t", c=5)
        cn_b = [Buf() for _ in range(5)]
        t1 = [A.alloc(512, F32) for _ in range(2)]
        t2 = [A.alloc(512, F32) for _ in range(2)]
        t1_b, t2_b = [Buf(), Buf()], [Buf(), Buf()]
        NSTG = 8
        stg = [A.alloc(512, BF16) for _ in range(NSTG)]
        stg_b = [Buf() for _ in range(NSTG)]
        va = A.alloc(4 * 520, BF16).rearrange("p (s n) -> p s n", s=4)
        va_sb = Buf()
        vbt = A.alloc(4 * 130, BF16).rearrange("p (s n) -> p s n", s=4)
        vb_sb = Buf()

        S.dma("sp", lambda e: e.dma_start(out=winx, in_=winx_s[l].rearrange("(k p) n -> p k n", p=128)), reads=wb[("winx", l)], writes=[w_b])
        S.dma("sp", lambda e: e.dma_start(out=wuqa, in_=wuqa_s[l].rearrange("(k p) n -> p k n", p=128)), reads=wb[("wuqa", l)], writes=[w_b])
        S.dma("sp", lambda e: e.dma_start(out=wuqb, in_=wuqb_s[l].rearrange("(k p) n -> p k n", p=128)), reads=wb[("wuqb", l)], writes=[w_b])
        S.dma("sp", lambda e: e.dma_start(out=wuk, in_=wuk_s[l].rearrange("(k p) n -> p k n", p=128)), reads=wb[("wuk", l)], writes=[w_b])
        S.dma("sp", lambda e: e.dma_start(out=wuv, in_=wuv_s[l].rearrange("(k p) n -> p k n", p=128)), reads=wb[("wuv", l)], writes=[w_b])
        S.dma("sp", lambda e: e.dma_start(out=gq, in_=gq_d[l]), writes=[w_b])
        S.dma("sp", lambda e: e.dma_start(out=gkv, in_=gkv_d[l]), writes=[w_b])
        load_gain(gmix, l, 2, gmix_b)
        S.op("dve", lambda e: e.memset(va, 1.0), writes=[va_sb])
        S.op("dve", lambda e: e.memset(vbt, 1.0), writes=[vb_sb])

        pp_state = [0]

        def pp():
            i = pp_state[0] % 6
            pp_state[0] += 1
            return banks[i], bank_b[i]

        stg_state = [0]

        def stage():
            i = stg_state[0] % NSTG
            stg_state[0] += 1
            return stg[i], stg_b[i]

        def load_x(t):
            S.dma("sp", lambda e: e.dma_start(out=xt[t % 2], in_=xtile_src(XS, t)), reads=[xs_b[t]], writes=xt_b[t % 2])
            S.dma("sp", lambda e: e.dma_start(out=cs[t % 2][64:96, :, :], in_=tabs_d[:, 64:96, t * TT:(t + 1) * TT].rearrange("c p t -> p c t")),
                  writes=[cs_b[t % 2]])

        def mm_group(out_ap, out_buf, items):
            n = len(items)
            for i, (lh, rh, rb) in enumerate(items):
                S.op("pe", lambda e, lh=lh, rh=rh, i=i: e.matmul(out_ap, lhsT=lh, rhs=rh, start=(i == 0), stop=(i == n - 1)),
                     reads=rb, writes=[out_buf])

        cp_state = [0]

        def copy_out(dst, src, rb, wbufs):
            if cp_state[0] % 2 == 0:
                S.op("act", lambda e: e.copy(out=dst, in_=src), reads=rb, writes=wbufs)
            else:
                S.op("dve", lambda e: e.tensor_copy(out=dst, in_=src), reads=rb, writes=wbufs)
            cp_state[0] += 1

        tcnt = [0]

        def rope(pa, pab, pbk, pbb, dst, dstb, cst, cstb):
            i = tcnt[0] % 2
            tcnt[0] += 1
            a1, a1b, a2, a2b = t1[i], t1_b[i], t2[i], t2_b[i]
            S.op("dve", lambda e: e.tensor_tensor(out=a1[64:96, :], in0=pa[64:96, :], in1=cst[64:96, 0, :], op=ALU.mult),
                 reads=[pab, cstb], writes=[a1b])
            S.op("dve", lambda e: e.tensor_tensor(out=a2[64:96, :], in0=pbk[64:96, :], in1=cst[64:96, 1, :], op=ALU.mult),
                 reads=[pbb, cstb], writes=[a2b])
            S.op("pool", lambda e: e.tensor_tensor(out=dst[64:96, :], in0=a1[64:96, :], in1=a2[64:96, :], op=ALU.add),
                 reads=[a1b, a2b], writes=[dstb])

        def sec_norm(t):
            X, Xb = xt[t % 2], xt_b[t % 2]
            for s in range(4):
                norm_rows(X[:, s, :], Xb[s], gmix, gmix_b, ss[s], ss_b[s], rs[s], rs_b[s], hb[:, s, :], hb_b[s], junk)
                transposes(hb[:, s, :], hb_b[s], hT[t % 2], hT_b[t % 2][s], s, 6 + s % 2)

        def sec_a(t):
            H, hall = hT[t % 2], hT_b[t % 2] + [w_b]
            for c in range(5):
                po, pob = pp()
                mm_group(po, pob, [(winx[:, k, c * 128:(c + 1) * 128], H[:, k, :], hall) for k in range(8)])
                S.op("dve", lambda e, c=c, po=po: e.tensor_copy(out=cT[:, c, :], in_=po), reads=[pob], writes=[cT_b[c]])
                S.op("act", lambda e, c=c, po=po: e.activation(out=sq[:, c, :], in_=po, func=AF.Square), reads=[pob], writes=[sq_b[c]])

        def sec_b(t):
            H, hall = hT[t % 2], hT_b[t % 2] + [w_b]
            tsl = slice(t * TT, (t + 1) * TT)
            pa, pab = pp()
            pbk, pbb = pp()
            mm_group(pa[0:96, :], pab, [(winx[:, k, 576:672], H[:, k, :], hall) for k in range(8)])
            mm_group(pbk[0:96, :], pbb, [(winx[:, k, 1440:1536], H[:, k, :], hall) for k in range(8)])
            kr, krb = stage()
            rope(pa, pab, pbk, pbb, kr, krb, cs[t % 2], cs_b[t % 2])
            S.dma("pool", lambda e, kr=kr, tsl=tsl: e.dma_start(out=KR[:, tsl], in_=kr[64:96, :]), reads=[krb], writes=[kr_b[t]])
            for hp in range(4):
                po, pob = pp()
                mm_group(po, pob, [(winx[:, k, 672 + hp * 128:672 + (hp + 1) * 128], H[:, k, :], hall) for k in range(8)])
                qs, qsb = stage()
                copy_out(qs, po, [pob], [qsb])
                for j in range(2):
                    h = hp * 2 + j
                    S.dma("pool", lambda e, h=h, j=j, qs=qs, tsl=tsl: e.dma_start(out=QB[h, :, tsl], in_=qs[j * 64:(j + 1) * 64, :]),
                          reads=[qsb], writes=[qb_b[h][t]])
            po, pob = pp()
            mm_group(po, pob, [(winx[:, k, 1184:1312], H[:, k, :], hall) for k in range(8)])
            ks, ksb = stage()
            copy_out(ks, po, [pob], [ksb])
            for g in range(2):
                S.dma("pool", lambda e, g=g, ks=ks, tsl=tsl: e.dma_start(out=KB[g, :, tsl], in_=ks[g * 64:(g + 1) * 64, :]), reads=[ksb], writes=[kb_b[g][t]])
            for s in range(4):
                po, pob = pp()
                mm_group(po[:, 0:128], pob, [(H[:, k, s * 128:(s + 1) * 128], winx[:, k, 1312:1440], hall) for k in range(8)])
                copy_out(vbt[:, s, :].rearrange("p (h d) -> p h d", d=65)[:, :, 0:64], po[:, 0:128].rearrange("p (h d) -> p h d", d=64), [pob], [vb_sb])
            S.dma("pool", lambda e, tsl=tsl: e.dma_start(out=VB[tsl, :].rearrange("(s p) n -> p s n", p=128), in_=vbt), reads=[vb_sb], writes=[vb_b[t]])

        def sec_c(t):
            for gi, (c0, c1, dim, gv) in enumerate(((0, 3, 384, gq), (3, 5, 256, gkv))):
                po, pob = pp()
                mm_group(po, pob, [(onesf, sq[:, c, :], [sq_b[c], const_b]) for c in range(c0, c1)])
                rsqrt_rows(rsc[:, gi, :], po, 1.0 / dim, [pob], [rsc_b[gi]])
                for c in range(c0, c1):
                    S.op("dve", lambda e, c=c, c0=c0, gv=gv, gi=gi: e.scalar_tensor_tensor(
                        out=cn[:, c, :], in0=cT[:, c, :], scalar=gv[:, c - c0:c - c0 + 1], in1=rsc[:, gi, :], op0=ALU.mult, op1=ALU.mult),
                        reads=[cT_b[c], rsc_b[gi], w_b], writes=[cn_b[c]])

        def sec_d(t):
            tsl = slice(t * TT, (t + 1) * TT)
            cqn = [cn_b[0], cn_b[1], cn_b[2], w_b]
            ckn = [cn_b[3], cn_b[4], w_b]
            for hp in range(4):
                po, pob = pp()
                mm_group(po, pob, [(wuk[:, c, hp * 128:(hp + 1) * 128], cn[:, 3 + c, :], ckn) for c in range(2)])
                kn, knb = stage()
                copy_out(kn, po, [pob], [knb])
                for j in range(2):
                    h = hp * 2 + j
                    S.dma("pool", lambda e, h=h, j=j, kn=kn, tsl=tsl: e.dma_start(out=KT[h, :, tsl], in_=kn[j * 64:(j + 1) * 64, :]),
                          reads=[knb], writes=[kt_b[h][t]])
            for s in range(4):
                po, pob = pp()
                mm_group(po, pob, [(cn[:, 3 + c, s * 128:(s + 1) * 128], wuv[:, c, :], ckn) for c in range(2)])
                copy_out(va[:, s, :].rearrange("p (h d) -> p h d", d=65)[:, :, 0:64], po.rearrange("p (h d) -> p h d", d=64), [pob], [va_sb])
            S.dma("pool", lambda e, tsl=tsl: e.dma_start(out=VA[tsl, :].rearrange("(s p) n -> p s n", p=128), in_=va), reads=[va_sb], writes=[va_b[t]])
            for h in range(8):
                pa, pab = pp()
                pbk, pbb = pp()
                mm_group(pa[0:96, :], pab, [(wuqa[:, c, h * 96:(h + 1) * 96], cn[:, c, :], cqn) for c in range(3)])
                mm_group(pbk[0:96, :], pbb, [(wuqb[:, c, h * 96:(h + 1) * 96], cn[:, c, :], cqn) for c in range(3)])
                qa, qab = stage()
                S.op("act", lambda e, qa=qa, pa=pa: e.copy(out=qa[0:64, :], in_=pa[0:64, :]), reads=[pab], writes=[qab])
                rope(pa, pab, pbk, pbb, qa, qab, cs[t % 2], cs_b[t % 2])
                S.dma("pool", lambda e, h=h, qa=qa, tsl=tsl: e.dma_start(out=QT[h, :, tsl], in_=qa[0:96, :]), reads=[qab], writes=[qt_b[h][t]])

        load_x(0)
        sec_norm(0)
        for t in range(NT):
            if t + 1 < NT:
                load_x(t + 1)
            sec_a(t)
            sec_b(t)
            if t + 1 < NT:
                sec_norm(t + 1)
            sec_c(t)
            sec_d(t)

    def mla_pass(l):
        S.barrier()
        A.reset()
        vall = A.alloc(64 * 520, BF16).rearrange("p (k n) -> p k n", k=64)
        vall_b = Buf()
        kt = [A.alloc(S_LEN, BF16) for _ in range(2)]
        qt = [A.alloc(S_LEN, BF16) for _ in range(2)]
        kq_b = [Buf(), Buf()]
        NPT = 4
        pt = [A.alloc(1024, BF16) for _ in range(NPT)]
        pt_b = [Buf() for _ in range(NPT)]
        NX = 3
        Xn = [A.alloc(1024, F32) for _ in range(NX)]
        Xn_b = [Buf() for _ in range(NX)]
        oT = [A.alloc(1024, BF16) for _ in range(NX)]
        oT_b = [Buf() for _ in range(NX)]
        scale = 96.0 ** -0.5
        vsrc = VA.rearrange("(k p) n -> p k n", p=128)
        for c in range(4):
            S.dma("sp", lambda e, c=c: e.dma_start(out=vall[:, c * 16:(c + 1) * 16, :], in_=vsrc[:, c * 16:(c + 1) * 16, :]),
                  reads=va_b, writes=[vall_b])

        def load_head(h):
            i = h % 2
            S.dma("sp", lambda e: e.dma_start(out=kt[i][0:64, :], in_=KT[h]), reads=kt_b[h], writes=[kq_b[i]])
            S.dma("sp", lambda e: e.dma_start(out=kt[i][64:96, :], in_=KR), reads=kr_b, writes=[kq_b[i]])
            S.dma("sp", lambda e: e.dma_start(out=qt[i][0:96, :], in_=QT[h]), reads=qt_b[h], writes=[kq_b[i]])

        PS = [(PSA[:, j * 1024:(j + 1) * 1024], [bank_b[2 * j], bank_b[2 * j + 1]]) for j in range(3)]
        po = [banks[6], banks[7]]
        pob = [bank_b[6], bank_b[7]]
        items = [(h, qc, i) for h in range(8) for qc in range(8) for i in range(64)]
        slot_ctr = [0]
        slot_of = {}

        def qk(j):
            h, qc, i = items[j]
            K, Q, kqb = kt[h % 2], qt[h % 2], kq_b[h % 2]
            q0 = qc * 1024
            sl = j % 3
            slot_of[j] = sl
            ps, psb = PS[sl]
            for half in range(2):
                S.op("pe", lambda e, i=i, half=half, ps=ps, K=K, Q=Q, q0=q0: e.matmul(
                    ps[:, half * 512:(half + 1) * 512], lhsT=K[0:96, i * 128:(i + 1) * 128],
                    rhs=Q[0:96, q0 + half * 512:q0 + (half + 1) * 512], start=True, stop=True),
                    reads=[kqb], writes=[psb[half]])

        pending = []

        def norm_head(h, qc, xi):
            X, Xb = Xn[xi], Xn_b[xi]
            for half in range(2):
                S.op("act", lambda e, X=X, half=half: e.copy(out=X[0:65, half * 512:(half + 1) * 512], in_=po[half][0:65, :]),
                     reads=[pob[half]], writes=[Xb])
            S.op("dve", lambda e, X=X: e.reciprocal(out=X[64:65, :], in_=X[64:65, :]), reads=[Xb], writes=[Xb])

        def norm_tail(h, qc, xi, jcur):
            X, Xb = Xn[xi], Xn_b[xi]
            o, ob = oT[xi], oT_b[xi]
            bc, bcb = PS[jcur % 3]
            for half in range(2):
                S.op("pe", lambda e, X=X, half=half, bc=bc: e.matmul(bc[0:64, half * 512:(half + 1) * 512], lhsT=esel[0:65, :],
                                                                     rhs=X[0:65, half * 512:(half + 1) * 512], start=True, stop=True),
                     reads=[Xb, const_b], writes=[bcb[half]])
            S.op("dve", lambda e, X=X, bc=bc, o=o: e.tensor_tensor(out=o[0:64, :], in0=X[0:64, :], in1=bc[0:64, :], op=ALU.mult),
                 reads=[Xb] + bcb, writes=[ob])
            kc = h // 2
            q0 = qc * 1024
            S.dma("pool", lambda e, o=o, h=h, q0=q0: e.dma_start(out=OT[h * 64:(h + 1) * 64, q0:q0 + 1024], in_=o[0:64, :]),
                  reads=[ob], writes=[ot_b[kc][2 * qc], ot_b[kc][2 * qc + 1]])

        LOOK = 2
        NI = len(items)
        load_head(0)
        loaded = {0}
        for j in range(min(LOOK, NI)):
            qk(j)
        ncnt = 0
        for j in range(NI):
            h, qc, i = items[j]
            if i == 0 and qc == 0 and h + 1 < 8 and (h + 1) not in loaded:
                load_head(h + 1)
                loaded.add(h + 1)
            if j + LOOK < NI:
                qk(j + LOOK)
            ps, psb = PS[slot_of[j]]
            pj = j % NPT
            S.op("act", lambda e, ps=ps, pj=pj: e.activation(out=pt[pj], in_=ps, func=AF.Exp, scale=scale), reads=psb, writes=[pt_b[pj]])
            for half in range(2):
                S.op("pe", lambda e, i=i, half=half, pj=pj, h=h: e.matmul(
                    po[half][0:65, :], lhsT=vall[:, i, h * 65:(h + 1) * 65],
                    rhs=pt[pj][:, half * 512:(half + 1) * 512], start=(i == 0), stop=(i == 63)),
                    reads=[vall_b, pt_b[pj]], writes=[pob[half]])
            if pending and pending[0][0] <= j:
                pending.pop(0)[1](j)
            if i == 63:
                xi = ncnt % NX
                ncnt += 1
                norm_head(h, qc, xi)
                pending.append((j + 10, lambda jcur, h=h, qc=qc, xi=xi: norm_tail(h, qc, xi, jcur)))
        for _, fn in pending:
            fn(NI - 1)

    def win_pass(l):
        S.barrier()
        A.reset()
        vb = A.alloc(64 * 130, BF16).rearrange("p (k n) -> p k n", k=64)
        vbb = Buf()
        bT1 = A.alloc(1536, F32)
        bT = [bT1, bT1]
        bT1_b = Buf()
        b8 = [A.alloc(1536, BF16) for _ in range(2)]
        bTb = [Buf(), Buf()]
        eskf = A.alloc(8, F32)
        zrow = A.alloc(128, F32)
        erow = A.alloc(1024, F32)
        eskb = Buf()
        kb = [A.alloc(S_LEN, BF16) for _ in range(2)]
        kbb = [Buf(), Buf()]
        qb = [A.alloc(4096, BF16).rearrange("p (h t) -> p h t", h=4) for _ in range(2)]
        qbb = [Buf(), Buf()]
        oW = [A.alloc(4096, BF16).rearrange("p (h t) -> p h t", h=4) for _ in range(2)]
        oWb = [Buf(), Buf()]
        pw = [A.alloc(1536, BF16) for _ in range(2)]
        pwb = [Buf(), Buf()]
        NXW = 3
        Xw = [A.alloc(512, F32) for _ in range(NXW)]
        Xwb = [Buf() for _ in range(NXW)]
        scale = 64.0 ** -0.5
        S.dma("sp", lambda e: e.dma_start(out=vb, in_=VB.rearrange("(k p) n -> p k n", p=128)), reads=vb_b, writes=[vbb])
        for g in range(2):
            S.dma("sp", lambda e, g=g: e.dma_start(out=bT[g], in_=biasT_d[g]), writes=[bT1_b])
            S.op("act", lambda e, g=g: e.mul(out=b8[g], in_=bT[g], mul=8.0), reads=[bT1_b], writes=[bTb[g]])
            S.dma("sp", lambda e, g=g: e.dma_start(out=kb[g][0:64, :], in_=KB[g]), reads=kb_b[g], writes=[kbb[g]])
        S.dma("sp", lambda e: e.dma_start(out=eskf[64:65, :], in_=sink_d[l:l + 1, :]), writes=[eskb])
        S.op("act", lambda e: e.activation(out=eskf[64:65, :], in_=eskf[64:65, :], func=AF.Exp), reads=[eskb], writes=[eskb])
        S.op("dve", lambda e: e.memset(zrow[64:65, :], 0.0), reads=[eskb], writes=[eskb])
        for hh in range(8):
            S.op("dve", lambda e, hh=hh: e.tensor_scalar(out=erow[64:65, hh * 128:(hh + 1) * 128], in0=zrow[64:65, :],
                                                         scalar1=eskf[64:65, hh:hh + 1], scalar2=None, op0=ALU.add), reads=[eskb], writes=[eskb])
        rhi = [A.alloc(512, BF16) for _ in range(NXW)]
        rlo = [A.alloc(512, BF16) for _ in range(NXW)]
        rhl_b = [Buf() for _ in range(NXW)]

        blocks = [(g, c, nb) for c in range(8) for g in range(2) for nb in range(8)]
        wo_load_x, wo_load_ot, wo_tile = wo_setup(l)
        NB = len(blocks)
        chunk_bufs = {}

        def load_chunk(g, c):
            ci = (c * 2 + g) % 2
            Qc, Qcb = qb[ci], qbb[ci]
            S.dma("sp", lambda e: e.dma_start(
                out=Qc[0:64, :, :], in_=QB[g * 4:(g + 1) * 4, :, c * 1024:(c + 1) * 1024].rearrange("h d t -> d h t")),
                reads=[qb_b[g * 4 + hh][2 * c + u] for hh in range(4) for u in range(2)], writes=[Qcb])

        def geom(bi):
            g, c, nb = blocks[bi]
            n = c * 8 + nb
            ms = [m for m in range(3) if 0 <= n - 1 + m < 64]
            base = 0 if bi % 2 == 0 else 3
            return g, c, nb, n, ms, base

        def s1(bi):
            g, c, nb, n, ms, base = geom(bi)
            if nb == 0:
                load_chunk(g, c)
            ci = (c * 2 + g) % 2
            Qc, Qcb = qb[ci], qbb[ci]
            for m in ms:
                ps, psb = banks[base + m], bank_b[base + m]
                kblk = n - 1 + m
                S.op("pe", lambda e, ps=ps, kblk=kblk, g=g, Qc=Qc, nb=nb: e.matmul(
                    ps.rearrange("p (h q) -> p h q", h=4), lhsT=kb[g][0:64, kblk * 128:(kblk + 1) * 128],
                    rhs=Qc[0:64, :, nb * 128:(nb + 1) * 128], start=True, stop=False), reads=[kbb[g], Qcb], writes=[psb])
                S.op("pe", lambda e, ps=ps, m=m, g=g: e.matmul(ps, lhsT=ident, rhs=b8[g][:, m * 512:(m + 1) * 512], start=False, stop=True),
                     reads=[bTb[g], const_b], writes=[psb])

        def s23(bi):
            g, c, nb, n, ms, base = geom(bi)
            ti = bi % 2
            lo, hi = ms[0] * 512, (ms[-1] + 1) * 512
            src = PSA[:, base * 512 + lo:base * 512 + hi]
            S.op("act", lambda e, ti=ti, lo=lo, hi=hi, src=src: e.activation(out=pw[ti][:, lo:hi], in_=src, func=AF.Exp, scale=scale),
                 reads=[bank_b[base + m] for m in ms], writes=[pwb[ti]])
            po, pob = banks[6], bank_b[6]
            for m in ms:
                kblk = n - 1 + m
                S.op("pe", lambda e, m=m, kblk=kblk, g=g, ti=ti, ms=ms, po=po: e.matmul(
                    po[0:65, :], lhsT=vb[:, kblk, g * 65:(g + 1) * 65], rhs=pw[ti][:, m * 512:(m + 1) * 512],
                    start=(m == ms[0]), stop=(m == ms[-1])), reads=[vbb, pwb[ti]], writes=[pob])
            xi = bi % NXW
            X, Xb = Xw[xi], Xwb[xi]
            S.op("act", lambda e, X=X, po=po: e.copy(out=X[0:64, :], in_=po[0:64, :]), reads=[pob], writes=[Xb])
            S.op("dve", lambda e, X=X, po=po, g=g: e.tensor_tensor(out=X[64:65, :], in0=po[64:65, :], in1=erow[64:65, g * 512:(g + 1) * 512], op=ALU.add),
                 reads=[pob, eskb], writes=[Xb])
            S.op("dve", lambda e, X=X: e.reciprocal(out=X[64:65, :], in_=X[64:65, :]), reads=[Xb], writes=[Xb])
            S.op("pool", lambda e, X=X, xi=xi: e.tensor_copy(out=rhi[xi][64:65, :], in_=X[64:65, :]), reads=[Xb], writes=[rhl_b[xi]])
            S.op("pool", lambda e, X=X, xi=xi: e.tensor_tensor(out=rlo[xi][64:65, :], in0=X[64:65, :], in1=rhi[xi][64:65, :], op=ALU.subtract),
                 reads=[Xb, rhl_b[xi]], writes=[rhl_b[xi]])

        def s4(bi):
            g, c, nb, n, ms, base = geom(bi)
            ci = (c * 2 + g) % 2
            O, Ob = oW[ci], oWb[ci]
            X, Xb = Xw[bi % NXW], Xwb[bi % NXW]
            bc, bcb = banks[7], bank_b[7]
            xi = bi % NXW
            S.op("pe", lambda e, xi=xi, bc=bc: e.matmul(bc[0:64, :], lhsT=eselb[64:65, :], rhs=rhi[xi][64:65, :], start=True, stop=False),
                 reads=[rhl_b[xi], const_b], writes=[bcb])
            S.op("pe", lambda e, xi=xi, bc=bc: e.matmul(bc[0:64, :], lhsT=eselb[64:65, :], rhs=rlo[xi][64:65, :], start=False, stop=True),
                 reads=[rhl_b[xi], const_b], writes=[bcb])
            S.op("dve", lambda e, X=X, O=O, nb=nb, bc=bc: e.tensor_tensor(
                out=O[0:64, :, nb * 128:(nb + 1) * 128], in0=X[0:64, :].rearrange("p (h q) -> p h q", h=4),
                in1=bc[0:64, :].rearrange("p (h q) -> p h q", h=4), op=ALU.mult), reads=[Xb, bcb], writes=[Ob])
            if nb == 7:
                r0 = 512 + g * 256
                S.dma("pool", lambda e, O=O, r0=r0, c=c: e.dma_start(
                    out=OT[r0:r0 + 256, c * 1024:(c + 1) * 1024].rearrange("(h d) t -> d h t", h=4), in_=O[0:64, :, :]),
                    reads=[Ob], writes=[ot_b[4 + g * 2][2 * c], ot_b[4 + g * 2][2 * c + 1], ot_b[5 + g * 2][2 * c], ot_b[5 + g * 2][2 * c + 1]])

        def after_s4(bi):
            g, c, nb = blocks[bi]
            if g == 1 and nb == 7:
                for t in (2 * c, 2 * c + 1):
                    wo_load_ot(t)
                for t in (2 * c, 2 * c + 1):
                    wo_tile(t)
                    if t + 2 < NT:
                        wo_load_x(t + 2)

        wo_load_x(0)
        wo_load_x(1)
        s1(0)
        for bi in range(NB):
            if bi + 1 < NB:
                s1(bi + 1)
            s23(bi)
            if bi >= 2:
                s4(bi - 2)
                after_s4(bi - 2)
        s4(NB - 2)
        after_s4(NB - 2)
        s4(NB - 1)
        after_s4(NB - 1)

    def wo_setup(l):
        wo = A.alloc(8 * 1024, BF16).rearrange("p (k n) -> p k n", k=8)
        wo_b = Buf()
        gm = A.alloc(1024, F32)
        gm_b = Buf()
        xt = [A.alloc(4096, F32).rearrange("p (s d) -> p s d", s=4) for _ in range(2)]
        xt_b = [[Buf() for _ in range(4)] for _ in range(2)]
        ot = [A.alloc(4096, BF16).rearrange("p (k t) -> p k t", k=8) for _ in range(2)]
        ott_b = [Buf(), Buf()]
        junk = A.alloc(1024, BF16)
        tn = [A.alloc(1024, F32) for _ in range(2)]
        tn_b = [Buf(), Buf()]
        ss = [A.alloc(1, F32) for _ in range(4)]
        rs = [A.alloc(1, F32) for _ in range(4)]
        ss_b = [Buf() for _ in range(4)]
        rs_b = [Buf() for _ in range(4)]
        S.dma("sp", lambda e: e.dma_start(out=wo, in_=wout_s[l].rearrange("(k p) n -> p k n", p=128)), reads=wb[("wout", l)], writes=[wo_b])
        load_gain(gm, l, 3, gm_b)

        def load_x(t):
            S.dma("sp", lambda e: e.dma_start(out=xt[t % 2], in_=xtile_src(XS, t)), reads=[xs_b[t]], writes=xt_b[t % 2])

        def load_ot(t):
            S.dma("sp", lambda e: e.dma_start(out=ot[t % 2], in_=OT[:, t * TT:(t + 1) * TT].rearrange("(k p) t -> p k t", p=128)),
                  reads=[ot_b[k][t] for k in range(8)], writes=[ott_b[t % 2]])

        def tile(t):
            for s in range(4):
                pyi = 4
                py = P45
                pyb = [bank_b[pyi], bank_b[pyi + 1]]
                for half in range(2):
                    for k in range(8):
                        S.op("pe", lambda e, s=s, half=half, k=k, py=py, t=t: e.matmul(
                            py[:, half * 512:(half + 1) * 512], lhsT=ot[t % 2][:, k, s * 128:(s + 1) * 128],
                            rhs=wo[:, k, half * 512:(half + 1) * 512], start=(k == 0), stop=(k == 7)),
                            reads=[ott_b[t % 2], wo_b], writes=[pyb[half]])
                post_norm_add(py, pyb, s, gm, gm_b, ss[s], ss_b[s], rs[s], rs_b[s], tn[s % 2], tn_b[s % 2],
                              xt[t % 2][:, s, :], xt_b[t % 2][s], junk)
            S.dma("pool", lambda e, t=t: e.dma_start(out=xtile_src(XS, t), in_=xt[t % 2]), reads=xt_b[t % 2], writes=[xs_b[t]])

        return load_x, load_ot, tile

    step = 0
    for l in range(L):
        stages = [
            lambda l=l: ffn_pass(l, 0, x_d if l == 0 else XS, xin_b if l == 0 else xs_b, XS, xs_b, 0, 1),
            lambda l=l: proj_pass(l),
            lambda l=l: mla_pass(l),
            lambda l=l: win_pass(l),
            lambda l=l: None,
            lambda l=l: ffn_pass(l, 1, XS, xs_b, out_d if l == L - 1 else XS, out_b if l == L - 1 else xs_b, 4, 5),
        ]
        for si, st in enumerate(stages):
            if l * 10 + si <= upto:
                st()
    S.emit()
    return nc


def _t5_bucket_np(rel):
    nb = 16
    max_exact = 8
    bucket = np.where(rel > 0, nb, 0)
    n = np.abs(rel)
    nf = np.maximum(n, 1).astype(np.float32)
    large = max_exact + (np.log(nf / max_exact) / np.log(128 / max_exact) * (nb - max_exact)).astype(np.int32)
    large = np.minimum(large, nb - 1)
    return bucket + np.where(n < max_exact, n, large)


def _host_prep(inp):
    f32 = np.float32
    w_in = np.asarray(inp["w_in"], f32)
    w_inx = np.concatenate([w_in, w_in[:, :, 576:640], w_in[:, :, 656:672], w_in[:, :, 640:656]], axis=2)
    w_uq = np.asarray(inp["mla_w_uq"], f32)
    wq4 = w_uq.reshape(L, 384, 8, 96)
    w_uqb = np.concatenate([wq4[..., 0:64], wq4[..., 80:96], wq4[..., 64:80]], axis=-1).reshape(L, 384, 768)
    wkv4 = np.asarray(inp["mla_w_ukv"], f32).reshape(L, 256, 8, 128)
    w_uk = np.ascontiguousarray(wkv4[..., 0:64]).reshape(L, 256, 512)
    w_uv = np.ascontiguousarray(wkv4[..., 64:128]).reshape(L, 256, 512)
    g_all = np.stack([np.asarray(inp[k], f32) for k in
                      ("ffn1_pre_g", "ffn1_post_g", "mix_pre_g", "mix_post_g", "ffn2_pre_g", "ffn2_post_g")], axis=1)
    gq = np.ascontiguousarray(np.asarray(inp["mla_q_norm_g"], f32).reshape(L, 3, 128).transpose(0, 2, 1))
    gkv = np.ascontiguousarray(np.asarray(inp["mla_kv_norm_g"], f32).reshape(L, 2, 128).transpose(0, 2, 1))
    pos = np.arange(S_LEN, dtype=np.float32)
    inv = (10000.0 ** (-np.arange(0, 32, 2, dtype=np.float32) / 32)).astype(np.float32)
    ang = pos[None, :] * inv[:, None]
    cos, sin = np.cos(ang).astype(f32), np.sin(ang).astype(f32)
    tabs = np.zeros((2, 128, S_LEN), f32)
    tabs[0, 64:80] = cos
    tabs[0, 80:96] = cos
    tabs[1, 64:80] = -sin
    tabs[1, 80:96] = sin
    rel_bias = np.asarray(inp["rel_bias"], f32)
    j = np.arange(128)[:, None]
    r = np.arange(128)[None, :]
    biasT = np.zeros((2, 128, 3, 4, 128), f32)
    for m in range(3):
        rel = (m - 1) * 128 + j - r
        bk = _t5_bucket_np(rel)
        ok = np.abs(rel) <= 128
        for g in range(2):
            for hh in range(4):
                vals = rel_bias[bk, g * 4 + hh]
                biasT[g, :, m, hh, :] = np.where(ok, vals, f32(-30000.0))
    biasT = biasT.reshape(2, 128, 1536)
    common = {
        "g_all": g_all, "gq": gq, "gkv": gkv, "sink": np.asarray(inp["swa_sink"], f32),
        "ffn1_w_gate": np.asarray(inp["ffn1_w_gate"], f32), "ffn2_w_gate": np.asarray(inp["ffn2_w_gate"], f32),
        "ffn1_w_up": np.asarray(inp["ffn1_w_up"], f32), "ffn2_w_up": np.asarray(inp["ffn2_w_up"], f32),
        "ffn1_w_down": np.asarray(inp["ffn1_w_down"], f32), "ffn2_w_down": np.asarray(inp["ffn2_w_down"], f32),
        "w_inx": np.ascontiguousarray(w_inx), "w_uqa": w_uq, "w_uqb": np.ascontiguousarray(w_uqb),
        "w_uk": w_uk, "w_uv": w_uv, "w_out": np.asarray(inp["w_out"], f32),
        "tabs": tabs, "biasT": biasT, "eye": np.eye(128, dtype=f32),
    }
    return common


def kernel(**inputs):
    x = np.asarray(inputs["x"], np.float32)
    common = _host_prep(inputs)
    nc = build()
    in_maps = []
    for b in range(8):
        m = dict(common)
        m["x"] = np.ascontiguousarray(x[b])
        in_maps.append(m)
    res = run_bass_kernel_spmd(nc, in_maps, core_ids=list(range(8)))
    return np.stack([np.asarray(r["out"], np.float32) for r in res.results], axis=0)
```

```python
import os
import numpy as np
import concourse.bass as bass
import concourse.mybir as mybir
from concourse.bass_utils import run_bass_kernel_spmd

F32 = mybir.dt.float32
BF16 = mybir.dt.bfloat16
ALU = mybir.AluOpType
AF = mybir.ActivationFunctionType

ENGS = ("pe", "act", "dve", "pool", "sp")

L = 2
S_LEN = 8192
D = 1024
DFF = 2816
NF = 22
TT = 512
NT = S_LEN // TT
EPS = 1e-6
PARTS = os.environ.get('DBG_PARTS', 'cdeqrkvQKV')


class Buf:
    __slots__ = ("last_w", "readers", "excl")

    def __init__(self, excl=False):
        self.last_w = None
        self.readers = []
        self.excl = excl


class Rec:
    __slots__ = ("eng", "idx", "fn", "deps", "signal", "semval", "is_dma", "dsem", "dval", "prev_dma")

    def __init__(self, eng, idx, fn, deps, is_dma):
        self.eng = eng
        self.idx = idx
        self.fn = fn
        self.deps = deps
        self.signal = False
        self.semval = 0
        self.is_dma = is_dma
        self.dsem = None
        self.dval = 0
        self.prev_dma = None


class Sched:
    def __init__(self, nc, n_dma_sems=12):
        self.nc = nc
        self.ops = {e: [] for e in ENGS}
        self.n_dma_sems = n_dma_sems
        self.dma_hist = {e: [] for e in ENGS}

    @staticmethod
    def _merge(deps, rec):
        if rec.is_dma:
            deps[id(rec)] = rec
        else:
            cur = deps.get(rec.eng)
            if cur is None or cur.idx < rec.idx:
                deps[rec.eng] = rec

    def _add(self, eng, fn, reads, writes, is_dma):
        ex = [b for b in reads if b.excl]
        if ex:
            reads = [b for b in reads if not b.excl]
            writes = list(writes) + [b for b in ex if b not in writes]
        deps = {}
        for b in reads:
            if b.last_w is not None:
                self._merge(deps, b.last_w)
        for b in writes:
            if b.last_w is not None:
                self._merge(deps, b.last_w)
            for r in b.readers:
                self._merge(deps, r)
        rec = Rec(eng, len(self.ops[eng]), fn, list(deps.values()), is_dma)
        self.ops[eng].append(rec)
        for b in reads:
            if not is_dma:
                rl = b.readers
                for i, r in enumerate(rl):
                    if (not r.is_dma) and r.eng == eng:
                        rl[i] = rec
                        break
                else:
                    rl.append(rec)
            else:
                b.readers.append(rec)
        for b in writes:
            b.last_w = rec
            b.readers = []
        return rec

    def op(self, eng, fn, reads=(), writes=()):
        return self._add(eng, fn, reads, writes, False)

    def dma(self, eng, fn, reads=(), writes=()):
        rec = self._add(eng, fn, reads, writes, True)
        hist = self.dma_hist[eng]
        k = len(hist)
        rec.dsem = k % self.n_dma_sems
        rec.dval = 16 * (k // self.n_dma_sems + 1)
        if k >= self.n_dma_sems:
            rec.prev_dma = hist[k - self.n_dma_sems]
        hist.append(rec)
        return rec

    def barrier(self):
        deps = []
        for e in ENGS:
            for r in reversed(self.ops[e]):
                if (not r.is_dma) and r.fn is not None:
                    deps.append(r)
                    break
            deps += self.dma_hist[e][-self.n_dma_sems:]
        for e in ENGS:
            rec = Rec(e, len(self.ops[e]), None, list(deps), False)
            self.ops[e].append(rec)

    def emit(self):
        nc = self.nc
        for e in ENGS:
            for rec in self.ops[e]:
                for d in rec.deps:
                    if d.is_dma:
                        continue
                    if d.eng == rec.eng and rec.eng == "pe" and not rec.is_dma:
                        continue
                    d.signal = True
        for e in ENGS:
            c = 0
            for rec in self.ops[e]:
                if rec.signal and not rec.is_dma:
                    c += 1
                    rec.semval = c
        sems = {e: nc.alloc_semaphore(name=f"s_{e}") for e in ENGS}
        dsems = {e: [nc.alloc_semaphore(name=f"d_{e}{i}") for i in range(self.n_dma_sems)]
                 for e in ENGS if self.dma_hist[e]}
        engattr = {"pe": "tensor", "act": "scalar", "dve": "vector", "pool": "gpsimd", "sp": "sync"}
        sched = self

        def actions(e):
            known = {x: 0 for x in ENGS}
            dknown = {}
            for rec in sched.ops[e]:
                need = {}
                dneed = {}
                for d in rec.deps:
                    if d.is_dma:
                        key = (d.eng, d.dsem)
                        if dknown.get(key, 0) < d.dval:
                            dneed[key] = max(dneed.get(key, 0), d.dval)
                    else:
                        if d.eng == e and e == "pe" and rec.fn is not None:
                            continue
                        if known[d.eng] < d.semval:
                            need[d.eng] = max(need.get(d.eng, 0), d.semval)
                if rec.is_dma and rec.prev_dma is not None:
                    p = rec.prev_dma
                    key = (p.eng, p.dsem)
                    if dknown.get(key, 0) < p.dval:
                        dneed[key] = max(dneed.get(key, 0), p.dval)
                for x, v in need.items():
                    yield ("wait", x, v)
                    known[x] = v
                for key, v in dneed.items():
                    yield ("wait", key, v)
                    dknown[key] = v
                if rec.fn is None:
                    continue
                if rec.is_dma:
                    yield ("op", rec, (e, rec.dsem), 16)
                elif rec.signal:
                    yield ("op", rec, e, 1)
                else:
                    yield ("op", rec, None, 0)
            last = {}
            for r in sched.dma_hist[e]:
                last[r.dsem] = max(last.get(r.dsem, 0), r.dval)
            for si, v in last.items():
                if dknown.get((e, si), 0) < v:
                    yield ("wait", (e, si), v)

        if os.environ.get("DBG_SIM"):
            streams = {e: list(actions(e)) for e in ENGS}
            ptr = {e: 0 for e in ENGS}
            val = {}
            print("ops per engine", {e: len(streams[e]) for e in ENGS})
            while True:
                prog = False
                for e in ENGS:
                    st = streams[e]
                    while ptr[e] < len(st):
                        a = st[ptr[e]]
                        if a[0] == "wait":
                            if val.get(a[1], 0) >= a[2]:
                                ptr[e] += 1
                                prog = True
                            else:
                                break
                        else:
                            if a[2] is not None:
                                val[a[2]] = val.get(a[2], 0) + a[3]
                            ptr[e] += 1
                            prog = True
                if not prog:
                    break
            stuck = {e: (ptr[e], len(streams[e]), streams[e][ptr[e]][:3] if ptr[e] < len(streams[e]) else None) for e in ENGS}
            print("SIM end:", stuck)

        def sem_of(key):
            return sems[key] if isinstance(key, str) else dsems[key[0]][key[1]]

        def run_engine(e, eng):
            for a in actions(e):
                if a[0] == "wait":
                    eng.wait_ge(sem_of(a[1]), a[2])
                else:
                    ins = a[1].fn(eng)
                    if a[2] is not None:
                        ins.then_inc(sem_of(a[2]), a[3])

        with nc.Block() as block:
            for e in ENGS:
                if not sched.ops[e]:
                    continue
                getattr(block, engattr[e])(lambda eng, e=e: run_engine(e, eng))


class Arena:
    def __init__(self, ap, cap_bytes):
        self.ap = ap
        self.cap = cap_bytes
        self.off = 0

    def reset(self):
        self.off = 0

    def alloc(self, nelem, dt):
        nb = nelem * (4 if dt == F32 else 2)
        nb = (nb + 63) // 64 * 64
        a = self.off
        self.off += nb
        assert self.off <= self.cap, (self.off, self.cap)
        v = self.ap[:, a // 2:(a + nb) // 2]
        if dt == F32:
            v = v.bitcast(F32)
        return v[:, :nelem]


def build(debug=False, upto=99):
    nc = bass.Bass("TRN2", target_bir_lowering=False)
    S = Sched(nc)

    def din(name, shape, dt=F32):
        return nc.dram_tensor(name, list(shape), dt, kind="ExternalInput").ap()

    def dscr(name, shape, dt):
        return nc.dram_tensor(name, list(shape), dt, kind=("ExternalOutput" if debug else "Internal")).ap()

    x_d = din("x", [S_LEN, D])
    out_d = nc.dram_tensor("out", [S_LEN, D], F32, kind="ExternalOutput").ap()
    g_d = din("g_all", [L, 6, D])
    gq_d = din("gq", [L, 128, 3])
    gkv_d = din("gkv", [L, 128, 2])
    sink_d = din("sink", [L, 8])
    wgate_d = [din("ffn1_w_gate", [L, D, DFF]), din("ffn2_w_gate", [L, D, DFF])]
    wup_d = [din("ffn1_w_up", [L, D, DFF]), din("ffn2_w_up", [L, D, DFF])]
    wdown_d = [din("ffn1_w_down", [L, DFF, D]), din("ffn2_w_down", [L, DFF, D])]
    winx_d = din("w_inx", [L, D, 1536])
    wuqa_d = din("w_uqa", [L, 384, 768])
    wuqb_d = din("w_uqb", [L, 384, 768])
    wuk_d = din("w_uk", [L, 256, 512])
    wuv_d = din("w_uv", [L, 256, 512])
    wout_d = din("w_out", [L, D, D])
    tabs_d = din("tabs", [2, 128, S_LEN])
    biasT_d = din("biasT", [2, 128, 1536])
    eye_d = din("eye", [128, 128])

    def wscr(name, shape):
        return nc.dram_tensor(name, list(shape), BF16, kind="Internal").ap()
    wgu_s = [[wscr(f"wgu_s{l}{f}", [11, 128, 4096]) for f in range(2)] for l in range(L)]
    wd_s = [[wscr(f"wd_s{l}{f}", [DFF, D]) for f in range(2)] for l in range(L)]
    winx_s = [wscr(f"winx_s{l}", [D, 1536]) for l in range(L)]
    wuqa_s = [wscr(f"wuqa_s{l}", [384, 768]) for l in range(L)]
    wuqb_s = [wscr(f"wuqb_s{l}", [384, 768]) for l in range(L)]
    wuk_s = [wscr(f"wuk_s{l}", [256, 512]) for l in range(L)]
    wuv_s = [wscr(f"wuv_s{l}", [256, 512]) for l in range(L)]
    wout_s = [wscr(f"wout_s{l}", [D, D]) for l in range(L)]
    XS = dscr("XS", [S_LEN, D], F32)
    QT = dscr("QT", [8, 96, S_LEN], BF16)
    KT = dscr("KT", [8, 64, S_LEN], BF16)
    KR = dscr("KR", [32, S_LEN], BF16)
    VA = dscr("VA", [S_LEN, 520], BF16)
    QB = dscr("QB", [8, 64, S_LEN], BF16)
    KB = dscr("KB", [2, 64, S_LEN], BF16)
    VB = dscr("VB", [S_LEN, 130], BF16)
    OT = dscr("OT", [D, S_LEN], BF16)
    xs_b = [Buf() for _ in range(NT)]
    xin_b = [Buf() for _ in range(NT)]
    out_b = [Buf() for _ in range(NT)]
    qt_b = [[Buf() for _ in range(NT)] for _ in range(8)]
    kt_b = [[Buf() for _ in range(NT)] for _ in range(8)]
    kr_b = [Buf() for _ in range(NT)]
    va_b = [Buf() for _ in range(NT)]
    qb_b = [[Buf() for _ in range(NT)] for _ in range(8)]
    kb_b = [[Buf() for _ in range(NT)] for _ in range(2)]
    vb_b = [Buf() for _ in range(NT)]
    ot_b = [[Buf() for _ in range(NT)] for _ in range(8)]

    ident = nc.alloc_sbuf_tensor("ident", [128, 128], BF16).ap()
    identf = nc.alloc_sbuf_tensor("identf", [128, 128], F32).ap()
    esel = nc.alloc_sbuf_tensor("esel", [128, 64], F32).ap()
    onesf = nc.alloc_sbuf_tensor("onesf", [128, 128], BF16).ap()
    eselb = nc.alloc_sbuf_tensor("eselb", [128, 64], BF16).ap()
    epsb = nc.alloc_sbuf_tensor("epsb", [128, 1], F32).ap()
    ARENA_BYTES = 200 * 1024
    arena_t = nc.alloc_sbuf_tensor("arena", [128, ARENA_BYTES // 2], BF16).ap()
    A = Arena(arena_t, ARENA_BYTES)
    const_b = Buf()

    PSA = nc.alloc_psum_tensor("PSA", [128, 8 * 512], F32).ap()
    P01 = PSA[:, 0:1024]
    P23 = PSA[:, 1024:2048]
    P45 = PSA[:, 2048:3072]
    banks = [PSA[:, i * 512:(i + 1) * 512] for i in range(8)]
    bank_b = [Buf(excl=True) for _ in range(8)]

    S.dma("sp", lambda e: e.dma_start(out=identf, in_=eye_d), writes=[const_b])
    S.op("dve", lambda e: e.tensor_copy(out=ident, in_=identf), reads=[const_b], writes=[const_b])
    S.op("dve", lambda e: e.memset(esel, 0.0), writes=[const_b])
    S.op("dve", lambda e: e.memset(esel[64:65, :], 1.0), reads=[const_b], writes=[const_b])
    S.op("dve", lambda e: e.memset(onesf, 1.0), reads=[const_b], writes=[const_b])
    S.op("dve", lambda e: e.memset(eselb, 1.0), reads=[const_b], writes=[const_b])
    S.op("dve", lambda e: e.memset(epsb, EPS), reads=[const_b], writes=[const_b])

    wb = {}

    def cast_dma(key, dst, src):
        b = Buf()
        S.dma("pool", lambda e: e.dma_start(out=dst, in_=src), writes=[b])
        wb.setdefault(key, []).append(b)

    def conv_ffn(l, f):
        for gu, wsrc in enumerate((wgate_d[f], wup_d[f])):
            for k in range(8):
                src = wsrc[l, k * 128:(k + 1) * 128, :].rearrange("p (g j) -> p g j", g=11)
                dst = wgu_s[l][f].rearrange("g p (u k j) -> p g u k j", u=2, k=8)[:, :, gu, k, :]
                cast_dma(("wgu", l, f), dst, src)
        for r in range(0, DFF, 704):
            cast_dma(("wd", l, f), wd_s[l][f][r:r + 704, :], wdown_d[f][l, r:r + 704, :])

    def conv_proj(l):
        for r in range(0, D, 256):
            cast_dma(("winx", l), winx_s[l][r:r + 256, :], winx_d[l, r:r + 256, :])
        cast_dma(("wuqa", l), wuqa_s[l], wuqa_d[l])
        cast_dma(("wuqb", l), wuqb_s[l], wuqb_d[l])
        cast_dma(("wuk", l), wuk_s[l], wuk_d[l])
        cast_dma(("wuv", l), wuv_s[l], wuv_d[l])

    def conv_wout(l):
        for r in range(0, D, 256):
            cast_dma(("wout", l), wout_s[l][r:r + 256, :], wout_d[l, r:r + 256, :])

    conv_ffn(0, 0)
    conv_proj(0)

    def rsqrt_rows(out_ap, in_ap, scale, rbufs, wbufs):
        S.op("act", lambda e: e.activation(out=out_ap, in_=in_ap, func=AF.Sqrt, scale=scale, bias=epsb[0:out_ap.shape[0], :]),
             reads=rbufs, writes=wbufs)
        S.op("dve", lambda e: e.reciprocal(out=out_ap, in_=out_ap), reads=wbufs, writes=wbufs)

    def load_gain(dst, idx_l, idx_g, buf, half=False):
        S.dma("sp", lambda e: e.dma_start(out=dst, in_=g_d[idx_l, idx_g].partition_broadcast(128)), writes=[buf])
        if half:
            S.op("act", lambda e: e.mul(out=dst, in_=dst, mul=0.5), reads=[buf], writes=[buf])

    def norm_rows(xap, xbuf, gbc, gbuf, ss, ssb, rs, rsb, out_ap, out_buf, junk):
        S.op("act", lambda e: e.activation(out=junk, in_=xap, func=AF.Square, accum_out=ss), reads=[xbuf], writes=[ssb])
        rsqrt_rows(rs, ss, 1.0 / D, [ssb], [rsb])
        S.op("dve", lambda e: e.scalar_tensor_tensor(out=out_ap, in0=xap, scalar=rs, in1=gbc, op0=ALU.mult, op1=ALU.mult),
             reads=[xbuf, rsb, gbuf], writes=[out_buf])

    def transposes(hb_s, hb_buf, hT, hT_buf, s, pbank):
        pt = banks[pbank].bitcast(BF16).rearrange("p (k j) -> p k j", k=8)
        for k in range(8):
            S.op("pe", lambda e, k=k: e.transpose(out=pt[:, k, :], in_=hb_s[:, k * 128:(k + 1) * 128], identity=ident),
                 reads=[hb_buf, const_b], writes=[bank_b[pbank]])
        eng = "dve" if s % 2 == 0 else "act"
        if eng == "dve":
            S.op("dve", lambda e: e.tensor_copy(out=hT[:, :, s * 128:(s + 1) * 128], in_=pt), reads=[bank_b[pbank]], writes=[hT_buf])
        else:
            S.op("act", lambda e: e.copy(out=hT[:, :, s * 128:(s + 1) * 128], in_=pt), reads=[bank_b[pbank]], writes=[hT_buf])

    def post_norm_add(py, pyb, s, gbc, gbuf, ss, ssb, rs, rsb, tn, tnb, xs_ap, xs_buf, junk):
        S.op("act", lambda e: e.activation(out=junk, in_=py, func=AF.Square, accum_out=ss), reads=pyb, writes=[ssb])
        rsqrt_rows(rs, ss, 1.0 / D, [ssb], [rsb])
        S.op("dve", lambda e: e.scalar_tensor_tensor(out=tn, in0=py, scalar=rs, in1=gbc, op0=ALU.mult, op1=ALU.mult),
             reads=pyb + [rsb, gbuf], writes=[tnb])
        S.op("pool", lambda e: e.tensor_tensor(out=xs_ap, in0=xs_ap, in1=tn, op=ALU.add), reads=[tnb, xs_buf], writes=[xs_buf])

    def xtile_src(dram, t):
        return dram[t * TT:(t + 1) * TT, :].rearrange("(s p) d -> p s d", p=128)

    def ffn_pass(l, f, src, src_b, dst, dst_b, gi_pre, gi_post):
        S.barrier()
        A.reset()
        wd = A.alloc(NF * 1024, BF16).rearrange("p (f n) -> p f n", f=NF)
        wd_b = Buf()
        NW = 4
        wgu = [A.alloc(4096, BF16).rearrange("p (u k j) -> p u k j", u=2, k=8) for _ in range(NW)]
        wgu_b = [Buf() for _ in range(NW)]
        xt = [A.alloc(4096, F32).rearrange("p (s d) -> p s d", s=4) for _ in range(2)]
        xt_b = [[Buf() for _ in range(4)] for _ in range(2)]
        hb = A.alloc(4096, BF16).rearrange("p (s d) -> p s d", s=4)
        hb_b = [Buf() for _ in range(4)]
        hT = [A.alloc(4096, BF16).rearrange("p (k t) -> p k t", k=8) for _ in range(2)]
        hT_b = [[Buf() for _ in range(4)] for _ in range(2)]
        aT = A.alloc(NF * 512, BF16).rearrange("p (f t) -> p f t", f=NF)
        aT_b = [Buf() for _ in range(NF)]
        gpre = A.alloc(1024, F32)
        gpost = A.alloc(1024, F32)
        gpre_b, gpost_b = Buf(), Buf()
        junk = A.alloc(1024, BF16)
        sg = [A.alloc(512, F32) for _ in range(2)]
        sg_b = [Buf(), Buf()]
        tn = [A.alloc(1024, F32) for _ in range(2)]
        tn_b = [Buf(), Buf()]
        ssA = [[A.alloc(1, F32) for _ in range(4)] for _ in range(2)]
        rsA = [[A.alloc(1, F32) for _ in range(4)] for _ in range(2)]
        ssA_b = [[Buf() for _ in range(4)] for _ in range(2)]
        rsA_b = [[Buf() for _ in range(4)] for _ in range(2)]
        ssB = [A.alloc(1, F32) for _ in range(4)]
        rsB = [A.alloc(1, F32) for _ in range(4)]
        ssB_b = [Buf() for _ in range(4)]
        rsB_b = [Buf() for _ in range(4)]

        wsrc = wd_s[l][f].rearrange("(f p) n -> p f n", p=128)
        for c in range(2):
            S.dma("sp", lambda e, c=c: e.dma_start(out=wd[:, c * 11:(c + 1) * 11, :], in_=wsrc[:, c * 11:(c + 1) * 11, :]),
                  reads=wb[("wd", l, f)], writes=[wd_b])
        load_gain(gpre, l, gi_pre, gpre_b)
        load_gain(gpost, l, gi_post, gpost_b, half=True)

        def load_x(t):
            S.dma("sp", lambda e: e.dma_start(out=xt[t % 2], in_=xtile_src(src, t)), reads=[src_b[t]], writes=xt_b[t % 2])

        def load_w(q):
            g = q % 11
            S.dma("sp", lambda e: e.dma_start(out=wgu[q % NW].rearrange("p u k j -> p (u k j)"), in_=wgu_s[l][f][g]),
                  reads=wb[("wgu", l, f)], writes=[wgu_b[q % NW]])

        def norm_part(t):
            for s in range(4):
                norm_rows(xt[t % 2][:, s, :], xt_b[t % 2][s], gpre, gpre_b, ssA[t % 2][s], ssA_b[t % 2][s],
                          rsA[t % 2][s], rsA_b[t % 2][s], hb[:, s, :], hb_b[s], junk)

        def trans_part(t):
            for s in range(4):
                transposes(hb[:, s, :], hb_b[s], hT[t % 2], hT_b[t % 2][s], s, 6 + s % 2)

        NQ = NT * 11
        load_x(0)
        for q in range(min(3, NQ)):
            load_w(q)
        norm_part(0)
        trans_part(0)
        for t in range(NT):
            if t + 1 < NT:
                load_x(t + 1)
            for grp in range(11):
                q = t * 11 + grp
                if q + 3 < NQ:
                    load_w(q + 3)
                w = wgu[q % NW]
                for j in range(2):
                    fi = grp * 2 + j
                    pg, pgb = banks[fi % 2], bank_b[fi % 2]
                    pu, pub = banks[2 + fi % 2], bank_b[2 + fi % 2]
                    for k in range(8):
                        S.op("pe", lambda e, k=k, j=j, w=w, pg=pg, t=t: e.matmul(
                            pg, lhsT=w[:, 0, k, j * 128:(j + 1) * 128], rhs=hT[t % 2][:, k, :], start=(k == 0), stop=(k == 7)),
                            reads=[wgu_b[q % NW]] + hT_b[t % 2], writes=[pgb])
                    for k in range(8):
                        S.op("pe", lambda e, k=k, j=j, w=w, pu=pu, t=t: e.matmul(
                            pu, lhsT=w[:, 1, k, j * 128:(j + 1) * 128], rhs=hT[t % 2][:, k, :], start=(k == 0), stop=(k == 7)),
                            reads=[wgu_b[q % NW]] + hT_b[t % 2], writes=[pub])
                    sgi, sgb = sg[fi % 2], sg_b[fi % 2]
                    S.op("act", lambda e, pg=pg, sgi=sgi: e.activation(out=sgi, in_=pg, func=AF.Silu), reads=[pgb], writes=[sgb])
                    S.op("dve", lambda e, pu=pu, sgi=sgi, fi=fi: e.tensor_tensor(out=aT[:, fi, :], in0=sgi, in1=pu, op=ALU.mult),
                         reads=[sgb, pub], writes=[aT_b[fi]])
            if t + 1 < NT:
                norm_part(t + 1)
            for s in range(4):
                pyi = 4 if s % 2 == 0 else 0
                py = (P45 if s % 2 == 0 else P01)
                pyb = [bank_b[pyi], bank_b[pyi + 1]]
                for half in range(2):
                    for fi in range(NF):
                        S.op("pe", lambda e, s=s, half=half, fi=fi, py=py: e.matmul(
                            py[:, half * 512:(half + 1) * 512], lhsT=aT[:, fi, s * 128:(s + 1) * 128],
                            rhs=wd[:, fi, half * 512:(half + 1) * 512], start=(fi == 0), stop=(fi == NF - 1)),
                            reads=[aT_b[fi], wd_b], writes=[pyb[half]])
                post_norm_add(py, pyb, s, gpost, gpost_b, ssB[s], ssB_b[s], rsB[s], rsB_b[s], tn[s % 2], tn_b[s % 2],
                              xt[t % 2][:, s, :], xt_b[t % 2][s], junk)
            S.dma("pool", lambda e, t=t: e.dma_start(out=xtile_src(dst, t), in_=xt[t % 2]), reads=xt_b[t % 2], writes=[dst_b[t]])
            if t + 1 < NT:
                trans_part(t + 1)

    def proj_pass(l):
        S.barrier()
        A.reset()
        winx = A.alloc(8 * 1536, BF16).rearrange("p (k n) -> p k n", k=8)
        wuqa = A.alloc(3 * 768, BF16).rearrange("p (k n) -> p k n", k=3)
        wuqb = A.alloc(3 * 768, BF16).rearrange("p (k n) -> p k n", k=3)
        wuk = A.alloc(2 * 512, BF16).rearrange("p (k n) -> p k n", k=2)
        wuv = A.alloc(2 * 512, BF16).rearrange("p (k n) -> p k n", k=2)
        w_b = Buf()
        gq = A.alloc(3, F32)
        gkv = A.alloc(2, F32)
        gmix = A.alloc(1024, F32)
        gmix_b = Buf()
        xt = [A.alloc(4096, F32).rearrange("p (s d) -> p s d", s=4) for _ in range(2)]
        xt_b = [[Buf() for _ in range(4)] for _ in range(2)]
        cs = [A.alloc(1024, F32).rearrange("p (c t) -> p c t", c=2) for _ in range(2)]
        cs_b = [Buf(), Buf()]
        hb = A.alloc(4096, BF16).rearrange("p (s d) -> p s d", s=4)
        hb_b = [Buf() for _ in range(4)]
        hT = [A.alloc(4096, BF16).rearrange("p (k t) -> p k t", k=8) for _ in range(2)]
        hT_b = [[Buf() for _ in range(4)] for _ in range(2)]
        junk = A.alloc(1024, BF16)
        ss = [A.alloc(1, F32) for _ in range(4)]
        rs = [A.alloc(1, F32) for _ in range(4)]
        ss_b = [Buf() for _ in range(4)]
        rs_b = [Buf() for _ in range(4)]
        cT = A.alloc(5 * 512, F32).rearrange("p (c t) -> p c t", c=5)
        sq = A.alloc(5 * 512, BF16).rearrange("p (c t) -> p c t", c=5)
        cT_b = [Buf() for _ in range(5)]
        sq_b = [Buf() for _ in range(5)]
        rsc = A.alloc(2 * 512, F32).rearrange("p (c t) -> p c t", c=2)
        rsc_b = [Buf(), Buf()]
        cn = A.alloc(5 * 512, BF16).rearrange("p (c t) -> p c t", c=5)
        cn_b = [Buf() for _ in range(5)]
        t1 = [A.alloc(512, F32) for _ in range(2)]
        t2 = [A.alloc(512, F32) for _ in range(2)]
        t1_b, t2_b = [Buf(), Buf()], [Buf(), Buf()]
        NSTG = 8
        stg = [A.alloc(512, BF16) for _ in range(NSTG)]
        stg_b = [Buf() for _ in range(NSTG)]
        va = A.alloc(4 * 520, BF16).rearrange("p (s n) -> p s n", s=4)
        va_sb = Buf()
        vbt = A.alloc(4 * 130, BF16).rearrange("p (s n) -> p s n", s=4)
        vb_sb = Buf()

        S.dma("sp", lambda e: e.dma_start(out=winx, in_=winx_s[l].rearrange("(k p) n -> p k n", p=128)), reads=wb[("winx", l)], writes=[w_b])
        S.dma("sp", lambda e: e.dma_start(out=wuqa, in_=wuqa_s[l].rearrange("(k p) n -> p k n", p=128)), reads=wb[("wuqa", l)], writes=[w_b])
        S.dma("sp", lambda e: e.dma_start(out=wuqb, in_=wuqb_s[l].rearrange("(k p) n -> p k n", p=128)), reads=wb[("wuqb", l)], writes=[w_b])
        S.dma("sp", lambda e: e.dma_start(out=wuk, in_=wuk_s[l].rearrange("(k p) n -> p k n", p=128)), reads=wb[("wuk", l)], writes=[w_b])
        S.dma("sp", lambda e: e.dma_start(out=wuv, in_=wuv_s[l].rearrange("(k p) n -> p k n", p=128)), reads=wb[("wuv", l)], writes=[w_b])
        S.dma("sp", lambda e: e.dma_start(out=gq, in_=gq_d[l]), writes=[w_b])
        S.dma("sp", lambda e: e.dma_start(out=gkv, in_=gkv_d[l]), writes=[w_b])
        load_gain(gmix, l, 2, gmix_b)
        S.op("dve", lambda e: e.memset(va, 1.0), writes=[va_sb])
        S.op("dve", lambda e: e.memset(vbt, 1.0), writes=[vb_sb])

        pp_state = [0]

        def pp():
            i = pp_state[0] % 6
            pp_state[0] += 1
            return banks[i], bank_b[i]

        stg_state = [0]

        def stage():
            i = stg_state[0] % NSTG
            stg_state[0] += 1
            return stg[i], stg_b[i]

        def load_x(t):
            S.dma("sp", lambda e: e.dma_start(out=xt[t % 2], in_=xtile_src(XS, t)), reads=[xs_b[t]], writes=xt_b[t % 2])
            S.dma("sp", lambda e: e.dma_start(out=cs[t % 2][64:96, :, :], in_=tabs_d[:, 64:96, t * TT:(t + 1) * TT].rearrange("c p t -> p c t")),
                  writes=[cs_b[t % 2]])

        def mm_group(out_ap, out_buf, items):
            n = len(items)
            for i, (lh, rh, rb) in enumerate(items):
                S.op("pe", lambda e, lh=lh, rh=rh, i=i: e.matmul(out_ap, lhsT=lh, rhs=rh, start=(i == 0), stop=(i == n - 1)),
                     reads=rb, writes=[out_buf])

        cp_state = [0]

        def copy_out(dst, src, rb, wbufs):
            if cp_state[0] % 2 == 0:
                S.op("act", lambda e: e.copy(out=dst, in_=src), reads=rb, writes=wbufs)
            else:
                S.op("dve", lambda e: e.tensor_copy(out=dst, in_=src), reads=rb, writes=wbufs)
            cp_state[0] += 1

        tcnt = [0]

        def rope(pa, pab, pbk, pbb, dst, dstb, cst, cstb):
            i = tcnt[0] % 2
            tcnt[0] += 1
            a1, a1b, a2, a2b = t1[i], t1_b[i], t2[i], t2_b[i]
            S.op("dve", lambda e: e.tensor_tensor(out=a1[64:96, :], in0=pa[64:96, :], in1=cst[64:96, 0, :], op=ALU.mult),
                 reads=[pab, cstb], writes=[a1b])
            S.op("dve", lambda e: e.tensor_tensor(out=a2[64:96, :], in0=pbk[64:96, :], in1=cst[64:96, 1, :], op=ALU.mult),
                 reads=[pbb, cstb], writes=[a2b])
            S.op("pool", lambda e: e.tensor_tensor(out=dst[64:96, :], in0=a1[64:96, :], in1=a2[64:96, :], op=ALU.add),
                 reads=[a1b, a2b], writes=[dstb])

        def sec_norm(t):
            X, Xb = xt[t % 2], xt_b[t % 2]
            for s in range(4):
                norm_rows(X[:, s, :], Xb[s], gmix, gmix_b, ss[s], ss_b[s], rs[s], rs_b[s], hb[:, s, :], hb_b[s], junk)
                transposes(hb[:, s, :], hb_b[s], hT[t % 2], hT_b[t % 2][s], s, 6 + s % 2)

        def sec_a(t):
            H, hall = hT[t % 2], hT_b[t % 2] + [w_b]
            for c in range(5):
                po, pob = pp()
                mm_group(po, pob, [(winx[:, k, c * 128:(c + 1) * 128], H[:, k, :], hall) for k in range(8)])
                S.op("dve", lambda e, c=c, po=po: e.tensor_copy(out=cT[:, c, :], in_=po), reads=[pob], writes=[cT_b[c]])
                S.op("act", lambda e, c=c, po=po: e.activation(out=sq[:, c, :], in_=po, func=AF.Square), reads=[pob], writes=[sq_b[c]])

        def sec_b(t):
            H, hall = hT[t % 2], hT_b[t % 2] + [w_b]
            tsl = slice(t * TT, (t + 1) * TT)
            pa, pab = pp()
            pbk, pbb = pp()
            mm_group(pa[0:96, :], pab, [(winx[:, k, 576:672], H[:, k, :], hall) for k in range(8)])
            mm_group(pbk[0:96, :], pbb, [(winx[:, k, 1440:1536], H[:, k, :], hall) for k in range(8)])
            kr, krb = stage()
            rope(pa, pab, pbk, pbb, kr, krb, cs[t % 2], cs_b[t % 2])
            S.dma("pool", lambda e, kr=kr, tsl=tsl: e.dma_start(out=KR[:, tsl], in_=kr[64:96, :]), reads=[krb], writes=[kr_b[t]])
            for hp in range(4):
                po, pob = pp()
                mm_group(po, pob, [(winx[:, k, 672 + hp * 128:672 + (hp + 1) * 128], H[:, k, :], hall) for k in range(8)])
                qs, qsb = stage()
                copy_out(qs, po, [pob], [qsb])
                for j in range(2):
                    h = hp * 2 + j
                    S.dma("pool", lambda e, h=h, j=j, qs=qs, tsl=tsl: e.dma_start(out=QB[h, :, tsl], in_=qs[j * 64:(j + 1) * 64, :]),
                          reads=[qsb], writes=[qb_b[h][t]])
            po, pob = pp()
            mm_group(po, pob, [(winx[:, k, 1184:1312], H[:, k, :], hall) for k in range(8)])
            ks, ksb = stage()
            copy_out(ks, po, [pob], [ksb])
            for g in range(2):
                S.dma("pool", lambda e, g=g, ks=ks, tsl=tsl: e.dma_start(out=KB[g, :, tsl], in_=ks[g * 64:(g + 1) * 64, :]), reads=[ksb], writes=[kb_b[g][t]])
            for s in range(4):
                po, pob = pp()
                mm_group(po[:, 0:128], pob, [(H[:, k, s * 128:(s + 1) * 128], winx[:, k, 1312:1440], hall) for k in range(8)])
                copy_out(vbt[:, s, :].rearrange("p (h d) -> p h d", d=65)[:, :, 0:64], po[:, 0:128].rearrange("p (h d) -> p h d", d=64), [pob], [vb_sb])
            S.dma("pool", lambda e, tsl=tsl: e.dma_start(out=VB[tsl, :].rearrange("(s p) n -> p s n", p=128), in_=vbt), reads=[vb_sb], writes=[vb_b[t]])

        def sec_c(t):
            for gi, (c0, c1, dim, gv) in enumerate(((0, 3, 384, gq), (3, 5, 256, gkv))):
                po, pob = pp()
                mm_group(po, pob, [(onesf, sq[:, c, :], [sq_b[c], const_b]) for c in range(c0, c1)])
                rsqrt_rows(rsc[:, gi, :], po, 1.0 / dim, [pob], [rsc_b[gi]])
                for c in range(c0, c1):
                    S.op("dve", lambda e, c=c, c0=c0, gv=gv, gi=gi: e.scalar_tensor_tensor(
                        out=cn[:, c, :], in0=cT[:, c, :], scalar=gv[:, c - c0:c - c0 + 1], in1=rsc[:, gi, :], op0=ALU.mult, op1=ALU.mult),
                        reads=[cT_b[c], rsc_b[gi], w_b], writes=[cn_b[c]])

        def sec_d(t):
            tsl = slice(t * TT, (t + 1) * TT)
            cqn = [cn_b[0], cn_b[1], cn_b[2], w_b]
            ckn = [cn_b[3], cn_b[4], w_b]
            for hp in range(4):
                po, pob = pp()
                mm_group(po, pob, [(wuk[:, c, hp * 128:(hp + 1) * 128], cn[:, 3 + c, :], ckn) for c in range(2)])
                kn, knb = stage()
                copy_out(kn, po, [pob], [knb])
                for j in range(2):
                    h = hp * 2 + j
                    S.dma("pool", lambda e, h=h, j=j, kn=kn, tsl=tsl: e.dma_start(out=KT[h, :, tsl], in_=kn[j * 64:(j + 1) * 64, :]),
                          reads=[knb], writes=[kt_b[h][t]])
            for s in range(4):
                po, pob = pp()
                mm_group(po, pob, [(cn[:, 3 + c, s * 128:(s + 1) * 128], wuv[:, c, :], ckn) for c in range(2)])
                copy_out(va[:, s, :].rearrange("p (h d) -> p h d", d=65)[:, :, 0:64], po.rearrange("p (h d) -> p h d", d=64), [pob], [va_sb])
            S.dma("pool", lambda e, tsl=tsl: e.dma_start(out=VA[tsl, :].rearrange("(s p) n -> p s n", p=128), in_=va), reads=[va_sb], writes=[va_b[t]])
            for h in range(8):
                pa, pab = pp()
                pbk, pbb = pp()
                mm_group(pa[0:96, :], pab, [(wuqa[:, c, h * 96:(h + 1) * 96], cn[:, c, :], cqn) for c in range(3)])
                mm_group(pbk[0:96, :], pbb, [(wuqb[:, c, h * 96:(h + 1) * 96], cn[:, c, :], cqn) for c in range(3)])
                qa, qab = stage()
                S.op("act", lambda e, qa=qa, pa=pa: e.copy(out=qa[0:64, :], in_=pa[0:64, :]), reads=[pab], writes=[qab])
                rope(pa, pab, pbk, pbb, qa, qab, cs[t % 2], cs_b[t % 2])
                S.dma("pool", lambda e, h=h, qa=qa, tsl=tsl: e.dma_start(out=QT[h, :, tsl], in_=qa[0:96, :]), reads=[qab], writes=[qt_b[h][t]])

        load_x(0)
        sec_norm(0)
        for t in range(NT):
            if t + 1 < NT:
                load_x(t + 1)
            sec_a(t)
            sec_b(t)
            if t + 1 < NT:
                sec_norm(t + 1)
            sec_c(t)
            sec_d(t)

    def mla_pass(l):
        S.barrier()
        A.reset()
        conv_wout(l)
        conv_ffn(l, 1)
        if l + 1 < L:
            conv_ffn(l + 1, 0)
            conv_proj(l + 1)
        vall = A.alloc(64 * 520, BF16).rearrange("p (k n) -> p k n", k=64)
        vall_b = Buf()
        kt = [A.alloc(S_LEN, BF16) for _ in range(2)]
        qt = [A.alloc(S_LEN, BF16) for _ in range(2)]
        kq_b = [Buf(), Buf()]
        NPT = 4
        pt = [A.alloc(1024, BF16) for _ in range(NPT)]
        pt_b = [Buf() for _ in range(NPT)]
        NX = 3
        Xn = [A.alloc(1024, F32) for _ in range(NX)]
        Xn_b = [Buf() for _ in range(NX)]
        oT = [A.alloc(1024, BF16) for _ in range(NX)]
        oT_b = [Buf() for _ in range(NX)]
        scale = 96.0 ** -0.5
        vsrc = VA.rearrange("(k p) n -> p k n", p=128)
        for c in range(4):
            S.dma("sp", lambda e, c=c: e.dma_start(out=vall[:, c * 16:(c + 1) * 16, :], in_=vsrc[:, c * 16:(c + 1) * 16, :]),
                  reads=va_b, writes=[vall_b])

        def load_head(h):
            i = h % 2
            S.dma("sp", lambda e: e.dma_start(out=kt[i][0:64, :], in_=KT[h]), reads=kt_b[h], writes=[kq_b[i]])
            S.dma("sp", lambda e: e.dma_start(out=kt[i][64:96, :], in_=KR), reads=kr_b, writes=[kq_b[i]])
            S.dma("sp", lambda e: e.dma_start(out=qt[i][0:96, :], in_=QT[h]), reads=qt_b[h], writes=[kq_b[i]])

        PS = [(PSA[:, j * 1024:(j + 1) * 1024], [bank_b[2 * j], bank_b[2 * j + 1]]) for j in range(3)]
        po = [banks[6], banks[7]]
        pob = [bank_b[6], bank_b[7]]
        items = [(h, qc, i) for h in range(8) for qc in range(8) for i in range(64)]
        slot_ctr = [0]
        slot_of = {}

        def qk(j):
            h, qc, i = items[j]
            K, Q, kqb = kt[h % 2], qt[h % 2], kq_b[h % 2]
            q0 = qc * 1024
            sl = j % 3
            slot_of[j] = sl
            ps, psb = PS[sl]
            for half in range(2):
                S.op("pe", lambda e, i=i, half=half, ps=ps, K=K, Q=Q, q0=q0: e.matmul(
                    ps[:, half * 512:(half + 1) * 512], lhsT=K[0:96, i * 128:(i + 1) * 128],
                    rhs=Q[0:96, q0 + half * 512:q0 + (half + 1) * 512], start=True, stop=True),
                    reads=[kqb], writes=[psb[half]])

        pending = []

        def norm_head(h, qc, xi):
            X, Xb = Xn[xi], Xn_b[xi]
            for half in range(2):
                S.op("act", lambda e, X=X, half=half: e.copy(out=X[0:65, half * 512:(half + 1) * 512], in_=po[half][0:65, :]),
                     reads=[pob[half]], writes=[Xb])
            S.op("dve", lambda e, X=X: e.reciprocal(out=X[64:65, :], in_=X[64:65, :]), reads=[Xb], writes=[Xb])

        def norm_tail(h, qc, xi, jcur):
            X, Xb = Xn[xi], Xn_b[xi]
            o, ob = oT[xi], oT_b[xi]
            bc, bcb = PS[jcur % 3]
            for half in range(2):
                S.op("pe", lambda e, X=X, half=half, bc=bc: e.matmul(bc[0:64, half * 512:(half + 1) * 512], lhsT=esel[0:65, :],
                                                                     rhs=X[0:65, half * 512:(half + 1) * 512], start=True, stop=True),
                     reads=[Xb, const_b], writes=[bcb[half]])
            S.op("dve", lambda e, X=X, bc=bc, o=o: e.tensor_tensor(out=o[0:64, :], in0=X[0:64, :], in1=bc[0:64, :], op=ALU.mult),
                 reads=[Xb] + bcb, writes=[ob])
            kc = h // 2
            q0 = qc * 1024
            S.dma("pool", lambda e, o=o, h=h, q0=q0: e.dma_start(out=OT[h * 64:(h + 1) * 64, q0:q0 + 1024], in_=o[0:64, :]),
                  reads=[ob], writes=[ot_b[kc][2 * qc], ot_b[kc][2 * qc + 1]])

        LOOK = 2
        NI = len(items)
        load_head(0)
        loaded = {0}
        for j in range(min(LOOK, NI)):
            qk(j)
        ncnt = 0
        for j in range(NI):
            h, qc, i = items[j]
            if i == 0 and qc == 0 and h + 1 < 8 and (h + 1) not in loaded:
                load_head(h + 1)
                loaded.add(h + 1)
            if j + LOOK < NI:
                qk(j + LOOK)
            ps, psb = PS[slot_of[j]]
            pj = j % NPT
            S.op("act", lambda e, ps=ps, pj=pj: e.activation(out=pt[pj], in_=ps, func=AF.Exp, scale=scale), reads=psb, writes=[pt_b[pj]])
            for half in range(2):
                S.op("pe", lambda e, i=i, half=half, pj=pj, h=h: e.matmul(
                    po[half][0:65, :], lhsT=vall[:, i, h * 65:(h + 1) * 65],
                    rhs=pt[pj][:, half * 512:(half + 1) * 512], start=(i == 0), stop=(i == 63)),
                    reads=[vall_b, pt_b[pj]], writes=[pob[half]])
            if pending and pending[0][0] <= j:
                pending.pop(0)[1](j)
            if i == 63:
                xi = ncnt % NX
                ncnt += 1
                norm_head(h, qc, xi)
                pending.append((j + 10, lambda jcur, h=h, qc=qc, xi=xi: norm_tail(h, qc, xi, jcur)))
        for _, fn in pending:
            fn(NI - 1)

    def win_pass(l):
        S.barrier()
        A.reset()
        vb = A.alloc(64 * 130, BF16).rearrange("p (k n) -> p k n", k=64)
        vbb = Buf()
        bT = [A.alloc(1536, F32) for _ in range(2)]
        b8 = [A.alloc(1536, BF16) for _ in range(2)]
        bTb = [Buf(), Buf()]
        eskf = A.alloc(8, F32)
        zrow = A.alloc(128, F32)
        erow = A.alloc(1024, F32)
        ehi = A.alloc(1024, BF16)
        elo = A.alloc(1024, BF16)
        sinkv = A.alloc(65, BF16)
        eskb = Buf()
        kb = [A.alloc(S_LEN, BF16) for _ in range(2)]
        kbb = [Buf(), Buf()]
        qb = [A.alloc(4096, BF16).rearrange("p (h t) -> p h t", h=4) for _ in range(2)]
        qbb = [Buf(), Buf()]
        oW = [A.alloc(4096, BF16).rearrange("p (h t) -> p h t", h=4) for _ in range(2)]
        oWb = [Buf(), Buf()]
        pw = [A.alloc(1536, BF16) for _ in range(2)]
        pwb = [Buf(), Buf()]
        NXW = 3
        Xw = [A.alloc(512, F32) for _ in range(NXW)]
        Xwb = [Buf() for _ in range(NXW)]
        scale = 64.0 ** -0.5
        S.dma("sp", lambda e: e.dma_start(out=vb, in_=VB.rearrange("(k p) n -> p k n", p=128)), reads=vb_b, writes=[vbb])
        for g in range(2):
            S.dma("sp", lambda e, g=g: e.dma_start(out=bT[g], in_=biasT_d[g]), writes=[bTb[g]])
            S.op("act", lambda e, g=g: e.mul(out=b8[g], in_=bT[g], mul=8.0), reads=[bTb[g]], writes=[bTb[g]])
            S.dma("sp", lambda e, g=g: e.dma_start(out=kb[g][0:64, :], in_=KB[g]), reads=kb_b[g], writes=[kbb[g]])
        S.dma("sp", lambda e: e.dma_start(out=eskf[64:65, :], in_=sink_d[l:l + 1, :]), writes=[eskb])
        S.op("act", lambda e: e.activation(out=eskf[64:65, :], in_=eskf[64:65, :], func=AF.Exp), reads=[eskb], writes=[eskb])
        S.op("dve", lambda e: e.memset(zrow[64:65, :], 0.0), reads=[eskb], writes=[eskb])
        for hh in range(8):
            S.op("dve", lambda e, hh=hh: e.tensor_scalar(out=erow[64:65, hh * 128:(hh + 1) * 128], in0=zrow[64:65, :],
                                                         scalar1=eskf[64:65, hh:hh + 1], scalar2=None, op0=ALU.add), reads=[eskb], writes=[eskb])
        rhi = [A.alloc(512, BF16) for _ in range(NXW)]
        rlo = [A.alloc(512, BF16) for _ in range(NXW)]
        rhl_b = [Buf() for _ in range(NXW)]

        blocks = [(g, c, nb) for g in range(2) for c in range(8) for nb in range(8)]
        NB = len(blocks)
        chunk_bufs = {}

        def load_chunk(g, c):
            ci = (g * 8 + c) % 2
            Qc, Qcb = qb[ci], qbb[ci]
            S.dma("sp", lambda e: e.dma_start(
                out=Qc[0:64, :, :], in_=QB[g * 4:(g + 1) * 4, :, c * 1024:(c + 1) * 1024].rearrange("h d t -> d h t")),
                reads=[qb_b[g * 4 + hh][2 * c + u] for hh in range(4) for u in range(2)], writes=[Qcb])

        def geom(bi):
            g, c, nb = blocks[bi]
            n = c * 8 + nb
            ms = [m for m in range(3) if 0 <= n - 1 + m < 64]
            base = 0 if bi % 2 == 0 else 3
            return g, c, nb, n, ms, base

        def s1(bi):
            g, c, nb, n, ms, base = geom(bi)
            if nb == 0:
                load_chunk(g, c)
            ci = (g * 8 + c) % 2
            Qc, Qcb = qb[ci], qbb[ci]
            for m in ms:
                ps, psb = banks[base + m], bank_b[base + m]
                kblk = n - 1 + m
                S.op("pe", lambda e, ps=ps, kblk=kblk, g=g, Qc=Qc, nb=nb: e.matmul(
                    ps.rearrange("p (h q) -> p h q", h=4), lhsT=kb[g][0:64, kblk * 128:(kblk + 1) * 128],
                    rhs=Qc[0:64, :, nb * 128:(nb + 1) * 128], start=True, stop=False), reads=[kbb[g], Qcb], writes=[psb])
                S.op("pe", lambda e, ps=ps, m=m, g=g: e.matmul(ps, lhsT=ident, rhs=b8[g][:, m * 512:(m + 1) * 512], start=False, stop=True),
                     reads=[bTb[g], const_b], writes=[psb])

        def s23(bi):
            g, c, nb, n, ms, base = geom(bi)
            ti = bi % 2
            lo, hi = ms[0] * 512, (ms[-1] + 1) * 512
            src = PSA[:, base * 512 + lo:base * 512 + hi]
            S.op("act", lambda e, ti=ti, lo=lo, hi=hi, src=src: e.activation(out=pw[ti][:, lo:hi], in_=src, func=AF.Exp, scale=scale),
                 reads=[bank_b[base + m] for m in ms], writes=[pwb[ti]])
            po, pob = banks[6], bank_b[6]
            for m in ms:
                kblk = n - 1 + m
                S.op("pe", lambda e, m=m, kblk=kblk, g=g, ti=ti, ms=ms, po=po: e.matmul(
                    po[0:65, :], lhsT=vb[:, kblk, g * 65:(g + 1) * 65], rhs=pw[ti][:, m * 512:(m + 1) * 512],
                    start=(m == ms[0]), stop=(m == ms[-1])), reads=[vbb, pwb[ti]], writes=[pob])
            xi = bi % NXW
            X, Xb = Xw[xi], Xwb[xi]
            S.op("act", lambda e, X=X, po=po: e.copy(out=X[0:64, :], in_=po[0:64, :]), reads=[pob], writes=[Xb])
            S.op("dve", lambda e, X=X, po=po, g=g: e.tensor_tensor(out=X[64:65, :], in0=po[64:65, :], in1=erow[64:65, g * 512:(g + 1) * 512], op=ALU.add),
                 reads=[pob, eskb], writes=[Xb])
            S.op("dve", lambda e, X=X: e.reciprocal(out=X[64:65, :], in_=X[64:65, :]), reads=[Xb], writes=[Xb])
            S.op("pool", lambda e, X=X, xi=xi: e.tensor_copy(out=rhi[xi][64:65, :], in_=X[64:65, :]), reads=[Xb], writes=[rhl_b[xi]])
            S.op("pool", lambda e, X=X, xi=xi: e.tensor_tensor(out=rlo[xi][64:65, :], in0=X[64:65, :], in1=rhi[xi][64:65, :], op=ALU.subtract),
                 reads=[Xb, rhl_b[xi]], writes=[rhl_b[xi]])

        def s4(bi):
            g, c, nb, n, ms, base = geom(bi)
            ci = (g * 8 + c) % 2
            O, Ob = oW[ci], oWb[ci]
            X, Xb = Xw[bi % NXW], Xwb[bi % NXW]
            bc, bcb = banks[7], bank_b[7]
            xi = bi % NXW
            S.op("pe", lambda e, xi=xi, bc=bc: e.matmul(bc[0:64, :], lhsT=eselb[64:65, :], rhs=rhi[xi][64:65, :], start=True, stop=False),
                 reads=[rhl_b[xi], const_b], writes=[bcb])
            S.op("pe", lambda e, xi=xi, bc=bc: e.matmul(bc[0:64, :], lhsT=eselb[64:65, :], rhs=rlo[xi][64:65, :], start=False, stop=True),
                 reads=[rhl_b[xi], const_b], writes=[bcb])
            S.op("dve", lambda e, X=X, O=O, nb=nb, bc=bc: e.tensor_tensor(
                out=O[0:64, :, nb * 128:(nb + 1) * 128], in0=X[0:64, :].rearrange("p (h q) -> p h q", h=4),
                in1=bc[0:64, :].rearrange("p (h q) -> p h q", h=4), op=ALU.mult), reads=[Xb, bcb], writes=[Ob])
            if nb == 7:
                r0 = 512 + g * 256
                S.dma("pool", lambda e, O=O, r0=r0, c=c: e.dma_start(
                    out=OT[r0:r0 + 256, c * 1024:(c + 1) * 1024].rearrange("(h d) t -> d h t", h=4), in_=O[0:64, :, :]),
                    reads=[Ob], writes=[ot_b[4 + g * 2][2 * c], ot_b[4 + g * 2][2 * c + 1], ot_b[5 + g * 2][2 * c], ot_b[5 + g * 2][2 * c + 1]])

        s1(0)
        for bi in range(NB):
            if bi + 1 < NB:
                s1(bi + 1)
            s23(bi)
            if bi >= 2:
                s4(bi - 2)
        s4(NB - 2)
        s4(NB - 1)

    def wo_pass(l):
        S.barrier()
        A.reset()
        wo = A.alloc(8 * 1024, BF16).rearrange("p (k n) -> p k n", k=8)
        wo_b = Buf()
        gm = A.alloc(1024, F32)
        gm_b = Buf()
        xt = [A.alloc(4096, F32).rearrange("p (s d) -> p s d", s=4) for _ in range(2)]
        xt_b = [[Buf() for _ in range(4)] for _ in range(2)]
        ot = [A.alloc(4096, BF16).rearrange("p (k t) -> p k t", k=8) for _ in range(2)]
        ott_b = [Buf(), Buf()]
        junk = A.alloc(1024, BF16)
        tn = [A.alloc(1024, F32) for _ in range(2)]
        tn_b = [Buf(), Buf()]
        ss = [A.alloc(1, F32) for _ in range(4)]
        rs = [A.alloc(1, F32) for _ in range(4)]
        ss_b = [Buf() for _ in range(4)]
        rs_b = [Buf() for _ in range(4)]
        S.dma("sp", lambda e: e.dma_start(out=wo, in_=wout_s[l].rearrange("(k p) n -> p k n", p=128)), reads=wb[("wout", l)], writes=[wo_b])
        load_gain(gm, l, 3, gm_b)

        def load(t):
            S.dma("sp", lambda e: e.dma_start(out=xt[t % 2], in_=xtile_src(XS, t)), reads=[xs_b[t]], writes=xt_b[t % 2])
            S.dma("sp", lambda e: e.dma_start(out=ot[t % 2], in_=OT[:, t * TT:(t + 1) * TT].rearrange("(k p) t -> p k t", p=128)),
                  reads=[ot_b[k][t] for k in range(8)], writes=[ott_b[t % 2]])

        load(0)
        for t in range(NT):
            if t + 1 < NT:
                load(t + 1)
            for s in range(4):
                pyi = 4 if s % 2 == 0 else 0
                py = (P45 if s % 2 == 0 else P01)
                pyb = [bank_b[pyi], bank_b[pyi + 1]]
                for half in range(2):
                    for k in range(8):
                        S.op("pe", lambda e, s=s, half=half, k=k, py=py, t=t: e.matmul(
                            py[:, half * 512:(half + 1) * 512], lhsT=ot[t % 2][:, k, s * 128:(s + 1) * 128],
                            rhs=wo[:, k, half * 512:(half + 1) * 512], start=(k == 0), stop=(k == 7)),
                            reads=[ott_b[t % 2], wo_b], writes=[pyb[half]])
                post_norm_add(py, pyb, s, gm, gm_b, ss[s], ss_b[s], rs[s], rs_b[s], tn[s % 2], tn_b[s % 2],
                              xt[t % 2][:, s, :], xt_b[t % 2][s], junk)
            S.dma("pool", lambda e, t=t: e.dma_start(out=xtile_src(XS, t), in_=xt[t % 2]), reads=xt_b[t % 2], writes=[xs_b[t]])

    step = 0
    for l in range(L):
        stages = [
            lambda l=l: ffn_pass(l, 0, x_d if l == 0 else XS, xin_b if l == 0 else xs_b, XS, xs_b, 0, 1),
            lambda l=l: proj_pass(l),
            lambda l=l: mla_pass(l),
            lambda l=l: win_pass(l),
            lambda l=l: wo_pass(l),
            lambda l=l: ffn_pass(l, 1, XS, xs_b, out_d if l == L - 1 else XS, out_b if l == L - 1 else xs_b, 4, 5),
        ]
        for si, st in enumerate(stages):
            if l * 10 + si <= upto:
                st()
    S.emit()
    return nc


def _t5_bucket_np(rel):
    nb = 16
    max_exact = 8
    bucket = np.where(rel > 0, nb, 0)
    n = np.abs(rel)
    nf = np.maximum(n, 1).astype(np.float32)
    large = max_exact + (np.log(nf / max_exact) / np.log(128 / max_exact) * (nb - max_exact)).astype(np.int32)
    large = np.minimum(large, nb - 1)
    return bucket + np.where(n < max_exact, n, large)


def _host_prep(inp):
    f32 = np.float32
    w_in = np.asarray(inp["w_in"], f32)
    w_inx = np.concatenate([w_in, w_in[:, :, 576:640], w_in[:, :, 656:672], w_in[:, :, 640:656]], axis=2)
    w_uq = np.asarray(inp["mla_w_uq"], f32)
    wq4 = w_uq.reshape(L, 384, 8, 96)
    w_uqb = np.concatenate([wq4[..., 0:64], wq4[..., 80:96], wq4[..., 64:80]], axis=-1).reshape(L, 384, 768)
    wkv4 = np.asarray(inp["mla_w_ukv"], f32).reshape(L, 256, 8, 128)
    w_uk = np.ascontiguousarray(wkv4[..., 0:64]).reshape(L, 256, 512)
    w_uv = np.ascontiguousarray(wkv4[..., 64:128]).reshape(L, 256, 512)
    g_all = np.stack([np.asarray(inp[k], f32) for k in
                      ("ffn1_pre_g", "ffn1_post_g", "mix_pre_g", "mix_post_g", "ffn2_pre_g", "ffn2_post_g")], axis=1)
    gq = np.ascontiguousarray(np.asarray(inp["mla_q_norm_g"], f32).reshape(L, 3, 128).transpose(0, 2, 1))
    gkv = np.ascontiguousarray(np.asarray(inp["mla_kv_norm_g"], f32).reshape(L, 2, 128).transpose(0, 2, 1))
    pos = np.arange(S_LEN, dtype=np.float32)
    inv = (10000.0 ** (-np.arange(0, 32, 2, dtype=np.float32) / 32)).astype(np.float32)
    ang = pos[None, :] * inv[:, None]
    cos, sin = np.cos(ang).astype(f32), np.sin(ang).astype(f32)
    tabs = np.zeros((2, 128, S_LEN), f32)
    tabs[0, 64:80] = cos
    tabs[0, 80:96] = cos
    tabs[1, 64:80] = -sin
    tabs[1, 80:96] = sin
    rel_bias = np.asarray(inp["rel_bias"], f32)
    j = np.arange(128)[:, None]
    r = np.arange(128)[None, :]
    biasT = np.zeros((2, 128, 3, 4, 128), f32)
    for m in range(3):
        rel = (m - 1) * 128 + j - r
        bk = _t5_bucket_np(rel)
        ok = np.abs(rel) <= 128
        for g in range(2):
            for hh in range(4):
                vals = rel_bias[bk, g * 4 + hh]
                biasT[g, :, m, hh, :] = np.where(ok, vals, f32(-30000.0))
    biasT = biasT.reshape(2, 128, 1536)
    common = {
        "g_all": g_all, "gq": gq, "gkv": gkv, "sink": np.asarray(inp["swa_sink"], f32),
        "ffn1_w_gate": np.asarray(inp["ffn1_w_gate"], f32), "ffn2_w_gate": np.asarray(inp["ffn2_w_gate"], f32),
        "ffn1_w_up": np.asarray(inp["ffn1_w_up"], f32), "ffn2_w_up": np.asarray(inp["ffn2_w_up"], f32),
        "ffn1_w_down": np.asarray(inp["ffn1_w_down"], f32), "ffn2_w_down": np.asarray(inp["ffn2_w_down"], f32),
        "w_inx": np.ascontiguousarray(w_inx), "w_uqa": w_uq, "w_uqb": np.ascontiguousarray(w_uqb),
        "w_uk": w_uk, "w_uv": w_uv, "w_out": np.asarray(inp["w_out"], f32),
        "tabs": tabs, "biasT": biasT, "eye": np.eye(128, dtype=f32),
    }
    return common


def kernel(**inputs):
    x = np.asarray(inputs["x"], np.float32)
    common = _host_prep(inputs)
    nc = build()
    in_maps = []
    for b in range(8):
        m = dict(common)
        m["x"] = np.ascontiguousarray(x[b])
        in_maps.append(m)
    res = run_bass_kernel_spmd(nc, in_maps, core_ids=list(range(8)))
    return np.stack([np.asarray(r["out"], np.float32) for r in res.results], axis=0)
```
